# Optimizing a Trainium2 kernel written in Bass

```python
import math
import jax, jax.numpy as jnp
from jax import lax
import numpy as np

D_MODEL = 1024
BATCH = 4
SEQ = 8192
DEPTH = 1

MEM_LEN = 256
MLA_HEADS = 4
QK_NOPE_DIM = 128
QK_ROPE_DIM = 64
QK_DIM = QK_NOPE_DIM + QK_ROPE_DIM
V_HEAD_DIM = 128
Q_LORA_RANK = 384
KV_LORA_RANK = 256
MLA_WIDTH = MLA_HEADS * V_HEAD_DIM
ROPE_THETA = 10000.0
Q_BLOCK = 128
LRU_WIDTH = D_MODEL - MLA_WIDTH
LRU_BLOCKS = 8
LRU_BLOCK_DIM = LRU_WIDTH // LRU_BLOCKS
CONV_WIDTH = 4
LRU_C = 8.0
IN_COLS = Q_LORA_RANK + KV_LORA_RANK + QK_ROPE_DIM + 2 * LRU_WIDTH
XA_HEADS = 4
XA_HEAD_DIM = 128
XA_WIDTH = XA_HEADS * XA_HEAD_DIM
D_FF = 2816
EPS = 1e-6
NEG_INF = -1e30

kernel_name = "hymba_mla_rglru_macaron_sandwich"


def rmsnorm(x, g):
    xf = x.astype(jnp.float32)
    y = xf * lax.rsqrt(jnp.mean(xf * xf, axis=-1, keepdims=True) + EPS)
    return (y * g.astype(jnp.float32)).astype(x.dtype)


def swiglu(h, w_gu, w_down):
    g, u = jnp.split(h @ w_gu, 2, axis=-1)
    return (jax.nn.silu(g) * u) @ w_down


def rope_tables(positions):
    inv = ROPE_THETA ** (-jnp.arange(0, QK_ROPE_DIM, 2, dtype=jnp.float32) / QK_ROPE_DIM)
    ang = positions.astype(jnp.float32)[..., None] * inv
    return jnp.cos(ang), jnp.sin(ang)


def apply_rope(x, cos, sin):
    xf = x.astype(jnp.float32)
    x1, x2 = jnp.split(xf, 2, axis=-1)
    return jnp.concatenate([x1 * cos - x2 * sin, x2 * cos + x1 * sin], axis=-1).astype(x.dtype)


def mla(c_q, c_kv, k_pe, q_a_g, w_uq, kv_a_g, w_ukv, cos, sin):
    B, S, _ = c_q.shape
    q = (rmsnorm(c_q, q_a_g) @ w_uq).reshape(B, S, MLA_HEADS, QK_DIM)
    q_nope, q_pe = q[..., :QK_NOPE_DIM], q[..., QK_NOPE_DIM:]
    q_pe = apply_rope(q_pe, cos[:, :, None, :], sin[:, :, None, :])
    kv = (rmsnorm(c_kv, kv_a_g) @ w_ukv).reshape(B, S, MLA_HEADS, QK_NOPE_DIM + V_HEAD_DIM)
    k_nope, v = kv[..., :QK_NOPE_DIM], kv[..., QK_NOPE_DIM:]
    k_pe = apply_rope(k_pe, cos, sin)
    q = jnp.concatenate([q_nope, q_pe], axis=-1)
    k = jnp.concatenate(
        [k_nope, jnp.broadcast_to(k_pe[:, :, None, :], (B, S, MLA_HEADS, QK_ROPE_DIM))], axis=-1)
    scale = 1.0 / math.sqrt(QK_DIM)
    nb = S // Q_BLOCK
    q_blocks = q.reshape(B, nb, Q_BLOCK, MLA_HEADS, QK_DIM).transpose(1, 0, 2, 3, 4)
    key_pos = jnp.arange(S)

    def one_block(args):
        q_blk, i = args
        s = jnp.einsum('bqhd,bkhd->bhqk', q_blk, k,
                       preferred_element_type=jnp.float32) * scale
        q_pos = i * Q_BLOCK + jnp.arange(Q_BLOCK)
        s = jnp.where(key_pos[None, :] <= q_pos[:, None], s, NEG_INF)
        p = jax.nn.softmax(s, axis=-1).astype(v.dtype)
        return jnp.einsum('bhqk,bkhd->bqhd', p, v)

    o = lax.map(one_block, (q_blocks, jnp.arange(nb)))
    return o.transpose(1, 0, 2, 3, 4).reshape(B, S, MLA_WIDTH)


def rglru_group(u, gate, conv_w, conv_b, w_a, b_a, w_x, b_x, lam):
    B, S, W = u.shape
    u_pad = jnp.pad(u, ((0, 0), (CONV_WIDTH - 1, 0), (0, 0)))
    xc = conv_b
    for tap in range(CONV_WIDTH):
        xc = xc + u_pad[:, tap:tap + S] * conv_w[tap]
    xb = xc.reshape(B, S, LRU_BLOCKS, LRU_BLOCK_DIM)
    r = jax.nn.sigmoid(jnp.einsum('bsnd,nde->bsne', xb, w_a) + b_a).reshape(B, S, W)
    i = jax.nn.sigmoid(jnp.einsum('bsnd,nde->bsne', xb, w_x) + b_x).reshape(B, S, W)
    log_a = -LRU_C * r.astype(jnp.float32) * jax.nn.softplus(-lam.astype(jnp.float32))
    a = jnp.exp(log_a)
    b = jnp.sqrt(-jnp.expm1(2.0 * log_a)) * (i * xc).astype(jnp.float32)

    def combine(lhs, rhs):
        a1, b1 = lhs
        a2, b2 = rhs
        return a1 * a2, a2 * b1 + b2

    _, h = lax.associative_scan(combine, (a, b), axis=1)
    return h.astype(u.dtype) * jax.nn.gelu(gate)


def memory_xattn(h, mem_n, w_q, w_kv, w_o):
    B, S, _ = h.shape
    q = (h @ w_q).reshape(B, S, XA_HEADS, XA_HEAD_DIM)
    k, v = jnp.split(mem_n @ w_kv, 2, axis=-1)
    k = k.reshape(B, MEM_LEN, XA_HEADS, XA_HEAD_DIM)
    v = v.reshape(B, MEM_LEN, XA_HEADS, XA_HEAD_DIM)
    s = jnp.einsum('bshd,bmhd->bhsm', q, k,
                   preferred_element_type=jnp.float32) / math.sqrt(XA_HEAD_DIM)
    p = jax.nn.softmax(s, axis=-1).astype(v.dtype)
    o = jnp.einsum('bhsm,bmhd->bshd', p, v).reshape(B, S, XA_WIDTH)
    return o @ w_o


def setup_inputs(seed: int = 0) -> dict:
    key = jax.random.key(seed)
    ks = iter(jax.random.split(key, 40))
    f32 = jnp.float32

    def w(shape, fan_in):
        return jax.random.normal(next(ks), (DEPTH,) + shape, f32) * fan_in ** -0.5

    def gain(n):
        return 1.0 + 0.1 * jax.random.normal(next(ks), (DEPTH, n), f32)

    def bias(shape):
        return 0.01 * jax.random.normal(next(ks), (DEPTH,) + shape, f32)

    x = jax.random.normal(next(ks), (BATCH, SEQ, D_MODEL), f32)
    mem = jax.random.normal(next(ks), (BATCH, MEM_LEN, D_MODEL), f32)
    offset = jax.random.randint(next(ks), (BATCH, 1), 0, 1024, dtype=jnp.int32)
    positions = (jnp.arange(SEQ, dtype=jnp.int32)[None, :] + offset).astype(jnp.int32)
    u = jax.random.uniform(next(ks), (DEPTH, LRU_WIDTH), f32, 0.9, 0.999) ** (1.0 / LRU_C)
    rg_lambda = jnp.log(u) - jnp.log1p(-u)
    return {
        "x": x,
        "mem": mem,
        "positions": positions,
        "ffn1_pre_g": gain(D_MODEL),
        "ffn1_w_gu": w((D_MODEL, 2 * D_FF), D_MODEL),
        "ffn1_w_down": w((D_FF, D_MODEL), D_FF),
        "ffn1_post_g": gain(D_MODEL),
        "mix_pre_g": gain(D_MODEL),
        "w_in": w((D_MODEL, IN_COLS), D_MODEL),
        "q_a_norm_g": gain(Q_LORA_RANK),
        "w_uq": w((Q_LORA_RANK, MLA_HEADS * QK_DIM), Q_LORA_RANK),
        "kv_a_norm_g": gain(KV_LORA_RANK),
        "w_ukv": w((KV_LORA_RANK, MLA_HEADS * (QK_NOPE_DIM + V_HEAD_DIM)), KV_LORA_RANK),
        "conv_w": w((CONV_WIDTH, LRU_WIDTH), CONV_WIDTH),
        "conv_b": bias((LRU_WIDTH,)),
        "rg_w_a": w((LRU_BLOCKS, LRU_BLOCK_DIM, LRU_BLOCK_DIM), LRU_BLOCK_DIM),
        "rg_b_a": bias((LRU_BLOCKS, LRU_BLOCK_DIM)),
        "rg_w_x": w((LRU_BLOCKS, LRU_BLOCK_DIM, LRU_BLOCK_DIM), LRU_BLOCK_DIM),
        "rg_b_x": bias((LRU_BLOCKS, LRU_BLOCK_DIM)),
        "rg_lambda": rg_lambda,
        "w_out": w((D_MODEL, D_MODEL), D_MODEL),
        "mix_post_g": gain(D_MODEL),
        "xa_pre_g": gain(D_MODEL),
        "mem_norm_g": gain(D_MODEL),
        "xa_w_q": w((D_MODEL, XA_WIDTH), D_MODEL),
        "xa_w_kv": w((D_MODEL, 2 * XA_WIDTH), D_MODEL),
        "xa_w_o": w((XA_WIDTH, D_MODEL), XA_WIDTH),
        "xa_post_g": gain(D_MODEL),
        "ffn2_pre_g": gain(D_MODEL),
        "ffn2_w_gu": w((D_MODEL, 2 * D_FF), D_MODEL),
        "ffn2_w_down": w((D_FF, D_MODEL), D_FF),
        "ffn2_post_g": gain(D_MODEL),
    }


def reference(x, mem, positions, ffn1_pre_g, ffn1_w_gu, ffn1_w_down, ffn1_post_g,
              mix_pre_g, w_in, q_a_norm_g, w_uq, kv_a_norm_g, w_ukv, conv_w, conv_b,
              rg_w_a, rg_b_a, rg_w_x, rg_b_x, rg_lambda, w_out, mix_post_g,
              xa_pre_g, mem_norm_g, xa_w_q, xa_w_kv, xa_w_o, xa_post_g,
              ffn2_pre_g, ffn2_w_gu, ffn2_w_down, ffn2_post_g):
    cos, sin = rope_tables(positions)
    o1 = Q_LORA_RANK
    o2 = o1 + KV_LORA_RANK
    o3 = o2 + QK_ROPE_DIM
    o4 = o3 + LRU_WIDTH
    for l in range(DEPTH):
        h = rmsnorm(x, ffn1_pre_g[l])
        x = x + 0.5 * rmsnorm(swiglu(h, ffn1_w_gu[l], ffn1_w_down[l]), ffn1_post_g[l])

        h = rmsnorm(x, mix_pre_g[l])
        z = h @ w_in[l]
        c_q, c_kv, k_pe = z[..., :o1], z[..., o1:o2], z[..., o2:o3]
        u, gate = z[..., o3:o4], z[..., o4:]
        y_mla = mla(c_q, c_kv, k_pe, q_a_norm_g[l], w_uq[l], kv_a_norm_g[l], w_ukv[l], cos, sin)
        y_lru = rglru_group(u, gate, conv_w[l], conv_b[l], rg_w_a[l], rg_b_a[l],
                            rg_w_x[l], rg_b_x[l], rg_lambda[l])
        y = jnp.concatenate([y_mla, y_lru], axis=-1) @ w_out[l]
        x = x + rmsnorm(y, mix_post_g[l])

        h = rmsnorm(x, xa_pre_g[l])
        mem_n = rmsnorm(mem, mem_norm_g[l])
        y = memory_xattn(h, mem_n, xa_w_q[l], xa_w_kv[l], xa_w_o[l])
        x = x + rmsnorm(y, xa_post_g[l])

        h = rmsnorm(x, ffn2_pre_g[l])
        x = x + 0.5 * rmsnorm(swiglu(h, ffn2_w_gu[l], ffn2_w_down[l]), ffn2_post_g[l])
    return x
```

```python
import math
from contextlib import ExitStack

import numpy as np
import concourse.bass as bass
import concourse.mybir as mybir
from concourse.bass_utils import run_bass_kernel_spmd

F32 = mybir.dt.float32
BF16 = mybir.dt.bfloat16
I32 = mybir.dt.int32
AF = mybir.ActivationFunctionType
ALU = mybir.AluOpType
AX = mybir.AxisListType

ENGS = ("pe", "act", "dve", "pool", "sp")
TB = 512
EPS = 1e-6
GELU_C = 0.044715
GELU_S = 2.0 * math.sqrt(2.0 / math.pi)
TWO_PI = 2.0 * math.pi


class Cfg:
    def __init__(self, D=1024, DFF=2816, NH=4, QL=384, KVL=256, LW=512, MEM=256, XH=4, S=8192,
                 n_cores=8):
        self.D, self.DFF, self.NH, self.QL, self.KVL, self.LW = D, DFF, NH, QL, KVL, LW
        self.MEM, self.XH, self.S, self.n_cores = MEM, XH, S, n_cores
        self.DC = D // 128
        self.NFC = DFF // 128
        self.QC = QL // 128
        self.KC = KVL // 128
        self.LC = LW // 128
        self.NHP = NH // 2
        self.MT = MEM // 128
        self.T = S // 2
        self.NTB = self.T // TB
        self.NKB = S // TB
        self.NKT = S // 128
        self.WINC = self.QC + self.KC + 1 + 2 * self.LC
        self.WUQC = NH + 2 * self.NHP
        assert NH * 128 + LW == D and XH * 128 <= D
        col = {}
        n = 0

        def add(name, w):
            nonlocal n
            col[name] = n
            n += w
        for g in ("g_f1pre", "g_f1post", "g_mixpre", "g_mixpost", "g_xapre", "g_mem", "g_xapost",
                  "g_f2pre", "g_f2post"):
            add(g, self.DC)
        add("g_qa", self.QC)
        add("g_kva", self.KC)
        for tap in range(4):
            add("convw%d" % tap, self.LC)
        add("convb", self.LC)
        add("b_a", self.LC)
        add("b_x", self.LC)
        add("lam", self.LC)
        add("invf", 1)
        add("nsg", 1)
        add("f0", 1)
        add("f1", 1)
        self.col = col
        self.NV = n


class Dep:
    __slots__ = ("name", "ws", "wd", "rs", "rd")

    def __init__(self, name=""):
        self.name = name
        self.ws = {}
        self.wd = []
        self.rs = {}
        self.rd = []


class Op:
    __slots__ = ("eng", "fn", "deps", "sig", "tok", "dma", "ndma", "inc", "ep")


class Prog:
    def __init__(self, nc, es):
        self.nc = nc
        self.es = es
        self.ops = {e: [] for e in ENGS}
        self.all = []
        self.epoch = 0
        self.esems = [{e: es.enter_context(nc.semaphore("s0_" + e)) for e in ENGS}]
        self.dsems = {}
        self.prev_compute = {}

    def dsem(self, name):
        if name not in self.dsems:
            self.dsems[name] = [self.es.enter_context(self.nc.semaphore("d_" + name)), 0]
        return self.dsems[name]

    def op(self, eng, fn, reads=(), writes=(), dma=None, ndma=1, inc=16):
        o = Op()
        o.eng, o.fn, o.sig, o.dma, o.ndma, o.tok = eng, fn, False, dma, ndma, None
        o.inc = inc
        o.ep = self.epoch
        deps = set()
        is_dma = dma is not None
        for d in reads:
            deps.update(d.ws.values())
            deps.update(d.wd)
        for d in writes:
            deps.update(d.ws.values())
            deps.update(d.wd)
            deps.update(d.rs.values())
            deps.update(d.rd)
        if not is_dma and eng == "pe":
            deps = set(d for d in deps if d.dma is not None or d.eng != "pe")
        deps.discard(o)
        o.deps = deps
        for d in reads:
            if is_dma:
                d.rd.append(o)
            else:
                d.rs[eng] = o
        for d in writes:
            if d.rs or d.rd:
                d.ws, d.wd, d.rs, d.rd = {}, [], {}, []
            if is_dma:
                d.wd.append(o)
            else:
                d.ws[eng] = o
        if is_dma:
            s = self.dsem(dma)
            s[1] += inc * ndma
            o.tok = (s[0], s[1])
        self.ops[eng].append(o)
        self.all.append(o)
        return o

    def wait_all(self, eng, ops):
        o = Op()
        o.eng, o.fn, o.sig, o.dma, o.ndma, o.tok = eng, None, False, None, 0, None
        o.inc = 0
        o.ep = self.epoch
        o.deps = set(ops)
        self.ops[eng].append(o)
        self.all.append(o)

    def fence(self, new_epoch=False):
        last = []
        for e in ENGS:
            for o in reversed(self.ops[e]):
                if o.dma is None and o.fn is not None:
                    last.append(o)
                    break
        dmas = {}
        for o in self.all:
            if o.dma is not None and not o.dma.startswith("cast"):
                dmas[o.dma] = o
        for e in ENGS:
            self.wait_all(e, last + list(dmas.values()))
        if not new_epoch:
            return
        self.epoch += 1
        nxt = dict(self.esems[-1])
        for e in ("pe", "act", "dve"):
            nxt[e] = self.es.enter_context(self.nc.semaphore("s%d_%s" % (self.epoch, e)))
        self.esems.append(nxt)

    def finalize_and_emit(self, block):
        for o in self.all:
            for d in o.deps:
                if d.dma is None:
                    d.sig = True
        for e in ENGS:
            n = {}
            for o in self.ops[e]:
                if o.dma is None and o.sig:
                    k = id(self.esems[o.ep][e])
                    n[k] = n.get(k, 0) + 1
                    o.tok = (self.esems[o.ep][e], n[k])
            print("engine", e, "signals per sem", sorted(n.values()))
        decos = {"pe": block.tensor, "act": block.scalar, "dve": block.vector,
                 "pool": block.gpsimd, "sp": block.sync}
        for e in ENGS:
            ops = self.ops[e]

            def body(eng, ops=ops):
                waited = {}
                for o in ops:
                    need = {}
                    for d in o.deps:
                        s, v = d.tok
                        k = id(s)
                        if k not in need or need[k][1] < v:
                            need[k] = (s, v)
                    for k, (s, v) in need.items():
                        if waited.get(k, 0) < v:
                            eng.wait_ge(s, v)
                            waited[k] = v
                    if o.fn is None:
                        continue
                    r = o.fn(eng)
                    if o.dma is not None:
                        assert len(r) == o.ndma
                        for ins in r:
                            ins.then_inc(o.tok[0], o.inc)
                    elif o.sig:
                        r.then_inc(o.tok[0], 1)
            decos[e](body)


class Arena:
    def __init__(self, handle, nelem):
        self.h, self.n, self.off = handle, nelem, 0

    def reset(self):
        self.off = 0

    def bf(self, n):
        n = (n + 1) // 2 * 2
        a = self.off
        self.off += n
        assert self.off <= self.n, ("arena overflow", self.off, self.n)
        return self.h[:, a:a + n]

    def f32(self, n):
        a = self.off
        self.off += 2 * n
        assert self.off <= self.n, ("arena overflow", self.off, self.n)
        return self.h[:, a:a + 2 * n].bitcast(F32)


def c3(ap, c):
    return ap.rearrange("p (c t) -> p c t", c=c)


class Builder:
    def __init__(self, cfg, debug=False):
        self.cfg = cfg
        self.debug = debug

    def mm(self, out, lhsT, rhs, start, stop, reads, writes):
        return self.P.op("pe", lambda e: e.matmul(out, lhsT=lhsT, rhs=rhs, start=start, stop=stop),
                         reads, writes)

    def act(self, out, in_, func, reads, writes, bias=None, scale=1.0, accum=None):
        def fn(e):
            kw = {}
            if bias is not None:
                kw["bias"] = bias
            if accum is not None:
                kw["accum_out"] = accum
            return e.activation(out=out, in_=in_, func=func, scale=scale, **kw)
        return self.P.op("act", fn, reads, writes)

    def tt(self, eng, out, in0, in1, op, reads, writes):
        return self.P.op(eng, lambda e: e.tensor_tensor(out=out, in0=in0, in1=in1, op=op), reads, writes)

    def ts(self, eng, out, in0, s1, s2, op0, op1, reads, writes):
        if s2 is None:
            return self.P.op(eng, lambda e: e.tensor_scalar(out=out, in0=in0, scalar1=s1, scalar2=None, op0=op0),
                             reads, writes)
        return self.P.op(eng, lambda e: e.tensor_scalar(out=out, in0=in0, scalar1=s1, scalar2=s2, op0=op0, op1=op1),
                         reads, writes)

    def stt(self, eng, out, in0, scalar, in1, op0, op1, reads, writes):
        return self.P.op(eng, lambda e: e.scalar_tensor_tensor(out=out, in0=in0, scalar=scalar, in1=in1,
                                                               op0=op0, op1=op1), reads, writes)

    def cp(self, eng, out, in_, reads, writes):
        if eng == "act":
            return self.P.op("act", lambda e: e.activation(out=out, in_=in_, func=AF.Copy), reads, writes)
        return self.P.op(eng, lambda e: e.tensor_copy(out=out, in_=in_), reads, writes)

    def dma(self, q, pairs, sem, reads, writes):
        return self.P.op(q, lambda e: [e.dma_start(out=o, in_=i) for (o, i) in pairs], reads, writes,
                         dma=sem, ndma=len(pairs))

    def bank(self, group):
        lst = self.bgroups[group]
        k = self.bpos.get(group, 0)
        self.bpos[group] = k + 1
        b = lst[k % len(lst)]
        return self.ps[b], self.psd[b]

    def rstd_from(self, srcs, sdeps, nfeat, N, out_rstd, out_dep, half=False):
        ps, pd = self.bank("sum")
        n = len(srcs)
        for c in range(n):
            k = self.sqpos % 3
            self.sqpos += 1
            sq, sqd = self.sq[k][:, 0:N], self.sqd[k]
            self.act(sq, srcs[c], AF.Square, [sdeps[c]], [sqd])
            self.mm(ps[:, 0:N], self.ones[:, :], sq, c == 0, c == n - 1, [sqd, self.constd], [pd])
        sc = (0.25 if half else 1.0)
        self.act(out_rstd, ps[:, 0:N], AF.Sqrt, [pd, self.constd], [out_dep], bias=self.eps4[:, 0:1] if half else self.eps1[:, 0:1],
                 scale=(4.0 if half else 1.0) / nfeat)
        self.P.op("dve", lambda e: e.reciprocal(out=out_rstd, in_=out_rstd), [out_dep], [out_dep])

    def load_w(self, slots, sdeps, k, dst_cols, src_ap, src_dep, semname):
        j = k % len(slots)
        dst = slots[j][:, 0:dst_cols]
        self.dma("sp", [(dst, src_ap)], "%s%d" % (semname, j), [src_dep], [sdeps[j]])
        return dst, sdeps[j]

    def norm_apply(self, out, outd, src, srcd, gcol, rst, rstd_dep, n):
        for k in range(n):
            self.stt("dve", out[:, k, :], src[:, k, :], self.pvec[:, gcol + k:gcol + k + 1], rst,
                     ALU.mult, ALU.mult, [srcd[k], rstd_dep], [outd[k]])

    def residual(self, x, xd, y, yd, gcol, rst, rstd_dep, n, scale):
        for k in range(n):
            self.stt("dve", y[:, k, :], y[:, k, :], self.pvec[:, gcol + k:gcol + k + 1], rst,
                     ALU.mult, ALU.mult, [yd[k], rstd_dep], [yd[k]])
            self.stt("dve", x[:, k, :], y[:, k, :], scale, x[:, k, :], ALU.mult, ALU.add,
                     [yd[k], xd[k]], [xd[k]])

    def build(self):
        c = self.cfg
        nc = bass.Bass("TRN2", target_bir_lowering=False)
        self.nc = nc
        DC, NFC, QC, KC, LC, NH, NHP, MT, XH = c.DC, c.NFC, c.QC, c.KC, c.LC, c.NH, c.NHP, c.MT, c.XH
        T, NTB, S, NKT, NKB = c.T, c.NTB, c.S, c.NKT, c.NKB
        col = c.col

        def din(name, shape, dt=F32):
            return nc.dram_tensor(name, list(shape), dt, kind="ExternalInput")

        def dscr(name, shape, dt, dbg=False):
            if dbg and self.debug:
                return nc.dram_tensor(name, list(shape), dt, kind="ExternalOutput")
            return nc.dram_tensor(name, list(shape), dt)

        xT = din("xT", [128, DC, T])
        pos = din("pos", [1, T], I32)
        memT = din("memT", [128, DC, c.MEM])
        pvec_d = din("pvec", [128, c.NV])
        mask_d = din("mask", [128, 8 * TB])
        wshapes = {
            "gu1": [NFC, 128, 2 * DC * 128], "dn1": [DC, 128, NFC * 128],
            "gu2": [NFC, 128, 2 * DC * 128], "dn2": [DC, 128, NFC * 128],
            "win": [1, 128, c.WINC * DC * 128], "wuq": [1, 128, c.WUQC * QC * 128],
            "wuk": [1, 128, NH * KC * 128], "wuv": [1, 128, KC * NH * 128],
            "wout": [1, 128, DC * DC * 128], "xwq": [1, 128, XH * DC * 128],
            "xwk": [1, 128, XH * DC * 128], "xwv": [1, 128, DC * XH * 128],
            "xwo": [1, 128, DC * XH * 128], "bda": [1, 128, LC * 128], "bdx": [1, 128, LC * 128],
        }
        self.wshapes = wshapes
        wf = {k: din("w_" + k, v) for k, v in wshapes.items()}
        wb = {k: dscr("wb_" + k, v, BF16) for k, v in wshapes.items()}
        outT = nc.dram_tensor("outT", [128, DC, T], F32, kind="ExternalOutput")
        x1s = dscr("x1s", [128, DC, T], F32, dbg=True)
        qns = dscr("qns", [128, NH, T], BF16, dbg=True)
        qps = dscr("qps", [128, NHP, T], BF16, dbg=True)
        us = dscr("us", [128, LC, T], F32, dbg=True)
        ggs = dscr("ggs", [128, LC, T], F32, dbg=True)
        ots = dscr("ots", [128, NH, T], BF16, dbg=True)
        yls = dscr("yls", [128, LC, T], BF16, dbg=True)
        GR = KC * 128 + 64
        NGP = max(1, T // 2048)
        TP = T // NGP
        BPP = TP // TB
        gin = [dscr("gin%d" % j, [GR, TP], BF16) for j in range(NGP)]
        gout = [dscr("gout%d" % j, [2 * GR, TP], BF16) for j in range(NGP)]
        HW_ = LC * NTB * 3
        hin = dscr("hin", [128, HW_], F32)
        hout = dscr("hout", [256, HW_], F32)
        SW_ = LC * NTB * 2
        sin_ = dscr("sin", [128, SW_], F32)
        sout = dscr("sout", [256, SW_], F32, dbg=True)
        pairs = [[2 * k, 2 * k + 1] for k in range(c.n_cores // 2)]

        es = ExitStack()
        self.es = es
        E = es.enter_context
        P = Prog(nc, es)
        self.P = P

        self.pvec = E(nc.sbuf_tensor("pvec_sb", [128, c.NV], F32))
        pvd = Dep("pvec")
        self.ones = E(nc.sbuf_tensor("ones", [128, 128], BF16))
        self.eps1 = E(nc.sbuf_tensor("eps1", [128, 1], F32))
        self.eps4 = E(nc.sbuf_tensor("eps4", [128, 1], F32))
        negpi = E(nc.sbuf_tensor("negpi", [128, 1], F32))
        cc = E(nc.sbuf_tensor("cc", [128, LC], F32))
        cc2 = E(nc.sbuf_tensor("cc2", [128, LC], F32))
        sp_t = E(nc.sbuf_tensor("sp_t", [128, 4 * LC], F32))
        Kx = E(nc.sbuf_tensor("Kx", [128, XH, c.MEM], BF16))
        Vx = E(nc.sbuf_tensor("Vx", [128, MT, XH * 128], BF16))
        bda = E(nc.sbuf_tensor("bda", [128, LC, 128], BF16))
        bdx = E(nc.sbuf_tensor("bdx", [128, LC, 128], BF16))
        halo = E(nc.sbuf_tensor("halo", [128, LC, NTB, 3], F32))
        hsel = E(nc.sbuf_tensor("hsel", [128, LC, NTB], F32))
        stl = E(nc.sbuf_tensor("stl", [128, LC, NTB, 2], F32))
        constd = Dep("const")
        self.constd = constd
        ccd = Dep("cc")
        kxd, vxd, bdd = Dep("kx"), Dep("vx"), Dep("bd")
        halod, hseld, stld = Dep("halo"), Dep("hsel"), Dep("stl")
        ARENA_N = self.arena_elems()
        arena_h = E(nc.sbuf_tensor("arena", [128, ARENA_N], BF16))
        A = Arena(arena_h, ARENA_N)
        self.ps = [E(nc.psum_tensor("ps%d" % i, [128, TB], F32))[:, :] for i in range(8)]
        self.psd = [Dep("ps%d" % i) for i in range(8)]
        self.bpos = {}
        self.sqpos = 0
        self.gucnt = 0
        self.dncnt = 0

        wdep = {}

        castslot = [Dep("castslot%d" % k) for k in range(4)]
        self.ncast = 0

        def cast(name, lo=None, hi=None):
            n0 = wshapes[name][0]
            lo = 0 if lo is None else lo
            hi = n0 if hi is None else hi
            d = Dep("w_%s_%d" % (name, lo))
            j = self.ncast % 4
            self.ncast += 1
            self.dma("pool", [(wb[name][lo:hi], wf[name][lo:hi])], "cast%d" % j, [], [d, castslot[j]])
            for k in range(lo, hi):
                wdep[(name, k)] = d
        self.dma("sp", [(self.pvec[:, :], pvec_d[:, :])], "os0", [], [pvd])
        for nm in ("bda", "bdx"):
            cast(nm)
        for lo in range(0, NFC, 2):
            cast("gu1", lo, min(NFC, lo + 2))
        for lo in range(0, DC, 2):
            cast("dn1", lo, min(DC, lo + 2))
        for nm in ("win", "wuq", "wuk", "wuv", "xwk", "xwv", "wout", "xwq", "xwo"):
            cast(nm)
        for lo in range(0, NFC, 2):
            cast("gu2", lo, min(NFC, lo + 2))
        for lo in range(0, DC, 2):
            cast("dn2", lo, min(DC, lo + 2))

        P.op("pool", lambda e: e.memset(self.ones[:, :], 1.0), [], [constd])
        P.op("pool", lambda e: e.memset(self.eps1[:, :], EPS), [], [constd])
        P.op("pool", lambda e: e.memset(self.eps4[:, :], 4.0 * EPS), [], [constd])
        P.op("pool", lambda e: e.memset(negpi[:, :], -math.pi), [], [constd])
        P.op("pool", lambda e: e.memset(halo[:, :, :, :], 0.0), [], [halod])
        self.bgroups = {"mm": [0, 1, 2, 3, 4, 5], "sum": [6, 7]}

        A.reset()
        self.sq = [A.bf(TB) for _ in range(3)]
        self.sqd = [Dep("sq%d" % i) for i in range(3)]
        self.tmpf = [A.f32(TB) for _ in range(2)]
        self.tmpfd = [Dep("tmpf%d" % i) for i in range(2)]
        rs_a, rs_ad = A.f32(TB), Dep("rs_a")
        common_mark = A.off
        lam = self.pvec[:, col["lam"]:col["lam"] + LC]
        e_ = sp_t[:, 0:LC]
        w_ = sp_t[:, LC:2 * LC]
        l_ = sp_t[:, 2 * LC:3 * LC]
        d_ = sp_t[:, 3 * LC:4 * LC]
        spd = Dep("sp")
        self.act(e_, lam, AF.Exp, [pvd], [spd], scale=-1.0)
        self.ts("dve", w_, e_, 1.0, None, ALU.add, None, [spd], [spd])
        self.act(l_, w_, AF.Ln, [spd], [spd])
        self.ts("dve", d_, w_, -1.0, 1e-30, ALU.add, ALU.max, [spd], [spd])
        P.op("dve", lambda e: e.reciprocal(out=d_, in_=d_), [spd], [spd])
        self.tt("dve", l_, l_, e_, ALU.mult, [spd], [spd])
        self.tt("dve", l_, l_, d_, ALU.mult, [spd], [spd])
        self.ts("dve", cc[:, :], l_, -8.0, None, ALU.mult, None, [spd], [ccd])
        self.ts("dve", cc2[:, :], l_, -16.0, None, ALU.mult, None, [spd], [ccd])

        self.dma("sp", [(bda[:, :, :], wb["bda"][0].rearrange("p (c m) -> p c m", c=LC))], "os4", [wdep[("bda", 0)]], [bdd])
        self.dma("sp", [(bdx[:, :, :], wb["bdx"][0].rearrange("p (c m) -> p c m", c=LC))], "os4", [wdep[("bdx", 0)]], [bdd])

        P.fence()
        self.wtag = "a"
        A.off = common_mark
        xbuf = [c3(A.f32(DC * TB), DC) for _ in range(2)]
        xbd = [[Dep("xb%d_%d" % (j, k)) for k in range(DC)] for j in range(2)]
        ybuf = c3(A.f32(DC * TB), DC)
        ybd = [Dep("yb%d" % k) for k in range(DC)]
        ust = [A.f32(TB) for _ in range(2)]
        ustd = [Dep("ust%d" % k) for k in range(2)]
        ggst = [A.f32(TB) for _ in range(2)]
        ggstd = [Dep("ggst%d" % k) for k in range(2)]
        posi = A.f32(TB).bitcast(I32)
        posf = A.f32(TB)
        tabA, tabB, mtmp = A.f32(TB), A.f32(TB), A.f32(TB)
        ang, kf = A.f32(TB), A.f32(TB)
        ki = A.f32(TB).bitcast(I32)
        tabd, posd = Dep("tab"), Dep("pos")
        rs_b, rs_bd = A.f32(TB), Dep("rs_b")
        rt = [A.f32(TB) for _ in range(2)]
        rtd = [Dep("rt%d" % k) for k in range(2)]
        gt = [A.f32(TB) for _ in range(2)]
        gtd = [Dep("gt%d" % k) for k in range(2)]
        hb = c3(A.bf(DC * TB), DC)
        hbd = [Dep("hb%d" % k) for k in range(DC)]
        act_t = c3(A.bf(NFC * TB), NFC)
        actd = [Dep("act%d" % k) for k in range(NFC)]
        cqn = c3(A.bf(QC * TB), QC)
        cqnd = [Dep("cqn%d" % k) for k in range(QC)]
        ckvn = c3(A.bf(KC * TB), KC)
        ckvnd = [Dep("ckvn%d" % k) for k in range(KC)]
        kr, krd = A.bf(TB), Dep("kr")
        qnst = c3(A.bf(NH * TB), NH)
        qnstd = Dep("qnst")
        qpst = c3(A.bf(NHP * TB), NHP)
        qpstd = Dep("qpst")
        self.guslot = [A.bf(2 * DC * 128) for _ in range(3)]
        self.guslotd = [Dep("gus%d" % k) for k in range(3)]
        self.dnslot = [A.bf(NFC * 128) for _ in range(2)]
        self.dnslotd = [Dep("dns%d" % k) for k in range(2)]
        win_sb = A.bf(c.WINC * DC * 128)
        wuq_sb = A.bf(c.WUQC * QC * 128)
        wind, wuqd = Dep("win"), Dep("wuq")
        win4 = win_sb.rearrange("p (m k n) -> p m k n", m=c.WINC, k=DC)
        wuq4 = wuq_sb.rearrange("p (m k n) -> p m k n", m=c.WUQC, k=QC)
        hin4 = hin.ap().rearrange("p (c i k) -> p c i k", c=LC, i=NTB)

        def load_x(i):
            j = i % 2
            self.dma("sp", [(xbuf[j], xT[:, :, i * TB:(i + 1) * TB])], "x%d" % j, [], xbd[j])

        load_x(0)
        for i in range(NTB):
            j = i % 2
            sl = slice(i * TB, (i + 1) * TB)
            xb, xd = xbuf[j], xbd[j]
            if i + 1 < NTB:
                load_x(i + 1)
            self.dma("sp", [(posi, bass.AP(pos, i * TB, [[0, 128], [1, TB]]))], "pos", [], [posd])
            self.cp("dve", posf, posi, [posd], [posd])
            C1 = 6.28125
            C2 = TWO_PI - 6.28125
            self.ts("dve", ang, posf, self.pvec[:, col["invf"]:col["invf"] + 1], None, ALU.mult, None,
                    [posd, pvd, tabd], [tabd])
            self.ts("dve", ki, ang, 1.0 / TWO_PI, None, ALU.mult, None, [tabd], [tabd])
            self.cp("dve", kf, ki, [tabd], [tabd])
            self.stt("dve", mtmp, kf, -C1, ang, ALU.mult, ALU.add, [tabd], [tabd])
            self.stt("dve", mtmp, kf, -C2, mtmp, ALU.mult, ALU.add, [tabd], [tabd])

            def wrap_sin(dst, src):
                self.ts("dve", kf, src, math.pi, -TWO_PI, ALU.is_gt, ALU.mult, [tabd], [tabd])
                self.tt("dve", src, src, kf, ALU.add, [tabd], [tabd])
                self.ts("dve", src, src, -math.pi, math.pi, ALU.max, ALU.min, [tabd], [tabd])
                self.act(dst, src, AF.Sin, [tabd], [tabd])
            self.ts("dve", ang, mtmp, 0.5 * math.pi, None, ALU.add, None, [tabd], [tabd])
            wrap_sin(tabB, mtmp)
            wrap_sin(tabA, ang)
            self.ts("dve", tabB, tabB, self.pvec[:, col["nsg"]:col["nsg"] + 1], None, ALU.mult, None, [tabd], [tabd])
            self.rstd_from([xb[:, k, :] for k in range(DC)], xd, c.D, TB, rs_a, rs_ad)
            self.norm_apply(hb, hbd, xb, xd, col["g_f1pre"], rs_a, rs_ad, DC)
            self._ffn_call(hb, hbd, "gu1", "dn1", wb, wdep, act_t, actd, ybuf, ybd, rs_b, rs_bd)
            if i == 0:
                self.dma("sp", [(win_sb, wb["win"][0])], "os0", [wdep[("win", 0)]], [wind])
                self.dma("sp", [(wuq_sb, wb["wuq"][0])], "os1", [wdep[("wuq", 0)]], [wuqd])
            self.residual(xb, xd, ybuf, ybd, col["g_f1post"], rs_b, rs_bd, DC, 1.0)
            self.dma("sp", [(x1s[:, :, sl], xb)], "sx%d" % j, xd, [])
            self.rstd_from([xb[:, k, :] for k in range(DC)], xd, c.D, TB, rs_a, rs_ad)
            self.norm_apply(hb, hbd, xb, xd, col["g_mixpre"], rs_a, rs_ad, DC)

            def proj(mc, ncol=128, c0=0):
                pz, pzd = self.bank("mm")
                for kc in range(DC):
                    self.mm(pz[0:ncol, :], win4[:, mc, kc, c0:c0 + ncol], hb[:, kc, :], kc == 0, kc == DC - 1,
                            [wind, hbd[kc]], [pzd])
                return pz, pzd
            for (n_, m0, gname, dst, dstd) in ((QC, 0, "g_qa", cqn, cqnd), (KC, QC, "g_kva", ckvn, ckvnd)):
                for k in range(n_):
                    pz, pzd = proj(m0 + k)
                    self.cp("act", ybuf[:, k, :], pz, [pzd], [ybd[k]])
                self.rstd_from([ybuf[:, k, :] for k in range(n_)], ybd, n_ * 128, TB, rs_b, rs_bd)
                self.norm_apply(dst, dstd, ybuf, ybd, col[gname], rs_b, rs_bd, n_)
            for k in range(KC):
                self.dma("sp", [(gin[i // BPP][k * 128:(k + 1) * 128, (i % BPP) * TB:(i % BPP + 1) * TB], ckvn[:, k, :])],
                         "gi%d" % k, [ckvnd[k]], [])
            mk = QC + KC
            p1, p1d = proj(mk, 64, 0)
            p2, p2d = proj(mk, 64, 64)
            self.tt("dve", rt[0][0:64, :], p1[0:64, :], tabA[0:64, :], ALU.mult, [p1d, tabd], [rtd[0]])
            self.tt("dve", rt[1][0:64, :], p2[0:64, :], tabB[0:64, :], ALU.mult, [p2d, tabd], [rtd[1]])
            self.tt("dve", kr[0:64, :], rt[0][0:64, :], rt[1][0:64, :], ALU.add, [rtd[0], rtd[1]], [krd])
            self.dma("sp", [(gin[i // BPP][KC * 128:KC * 128 + 64, (i % BPP) * TB:(i % BPP + 1) * TB], kr[0:64, :])],
                     "gikr", [krd], [])
            for k in range(LC):
                pz, pzd = proj(mk + 1 + k)
                u_, ud_ = ust[k % 2], ustd[k % 2]
                self.cp("act", u_, pz, [pzd], [ud_])
                self.dma("sp", [(us[:, k, sl], u_), (hin4[:, k, i, :], u_[:, TB - 3:TB])], "us%d" % (k % 2), [ud_], [])
            for k in range(LC):
                pz, pzd = proj(mk + 1 + LC + k)
                self.gelu_tanh(pz, pzd, gt[k % 2], gtd[k % 2], ggst[k % 2], ggstd[k % 2])
                self.dma("sp", [(ggs[:, k, sl], ggst[k % 2])], "gg%d" % (k % 2), [ggstd[k % 2]], [])
            def qproj(mc):
                pz, pzd = self.bank("mm")
                for kc in range(QC):
                    self.mm(pz, wuq4[:, mc, kc, :], cqn[:, kc, :], kc == 0, kc == QC - 1, [wuqd, cqnd[kc]], [pzd])
                return pz, pzd
            for h in range(NH):
                pz, pzd = qproj(h)
                self.cp("act", qnst[:, h, :], pz, [pzd], [qnstd])
            self.dma("sp", [(qns[:, :, sl], qnst)], "qn", [qnstd], [])
            for hp in range(NHP):
                p1, p1d = qproj(NH + hp)
                p2, p2d = qproj(NH + NHP + hp)
                self.tt("dve", rt[0], p1, tabA, ALU.mult, [p1d, tabd], [rtd[0]])
                self.tt("dve", rt[1], p2, tabB, ALU.mult, [p2d, tabd], [rtd[1]])
                self.tt("dve", qpst[:, hp, :], rt[0], rt[1], ALU.add, [rtd[0], rtd[1]], [qpstd])
            self.dma("sp", [(qps[:, :, sl], qpst)], "qp", [qpstd], [])

        P.fence(True)
        o1s = []
        for j in range(NGP):
            o1s.append(P.op("pool", lambda e, j=j: [e.collective_compute("AllGather", ALU.bypass, replica_groups=pairs,
                                                                         ins=[gin[j].ap().opt()], outs=[gout[j].ap().opt()])],
                            [], [], dma="cc_o1_%d" % j, ndma=1, inc=1))
        o2 = P.op("pool", lambda e: [e.collective_compute("AllGather", ALU.bypass, replica_groups=pairs,
                                                          ins=[hin.ap().opt()], outs=[hout.ap().opt()])], [], [],
                  dma="cc_o2", ndma=1, inc=1)
        P.wait_all("pool", o1s + [o2])
        P.fence()

        P.fence(True)
        A.reset()
        self.sq = [A.bf(TB) for _ in range(3)]
        self.tmpf = [A.f32(TB) for _ in range(2)]
        rs_a = A.f32(TB)
        common_mark = A.off
        self.bgroups = {"s": [0, 1, 2], "o": [3, 4], "l": [5], "gen": [7, 0, 1, 2], "lru": [6, 7], "sum": [6, 7],
                        "mm": [0, 1, 2, 3, 4, 5]}
        self.bpos = {}
        self.lru_group = "lru"
        f0 = self.pvec[:, col["f0"]:col["f0"] + 1]
        f1 = self.pvec[:, col["f1"]:col["f1"] + 1]
        ckvT = c3(A.bf(KC * S), KC)
        kpe = A.bf(S)
        Kn = A.bf(S)
        Vh = c3(A.bf(NKT * 128), NKT)
        wuk_sb = A.bf(NH * KC * 128)
        wuv_sb = A.bf(KC * NH * 128)
        qn_t = [A.bf(TB) for _ in range(2)]
        qp_t = [A.bf(TB) for _ in range(2)]
        qd = [Dep("q%d" % k) for k in range(2)]
        Pt = [A.bf(TB) for _ in range(4)]
        Ptd = [Dep("P%d" % k) for k in range(4)]
        mask = c3(A.bf(8 * TB), 8)
        ot_t = [A.bf(TB) for _ in range(2)]
        otd = [Dep("ot%d" % k) for k in range(2)]
        rec = [A.f32(TB) for _ in range(2)]
        recd = [Dep("rec%d" % k) for k in range(2)]
        ckvd, kped, knd, vvd, wukd, maskd = Dep("ckvT"), Dep("kpe"), Dep("Kn"), Dep("Vv"), Dep("wukv"), Dep("mask")
        hg = c3(A.f32(2 * LC * NTB * 3), 2).rearrange("p r (c i k) -> p r c i k", c=LC, i=NTB)
        hgd = Dep("hg")
        self.lru_alloc(A)
        sg_ = c3(A.f32(2 * LC * NTB * 2), 2).rearrange("p r (c i k) -> p r c i k", c=LC, i=NTB)
        sgd, sind, soutd = Dep("sg"), Dep("sin"), Dep("sout")
        NG = 2 * NTB
        Pg = c3(A.f32(LC * NG), LC)
        Hg = c3(A.f32(LC * NG), LC)
        HS = c3(A.f32(LC * (NG + 2)), LC)
        cd = Dep("carry")
        prs = []
        for j in range(NGP):
            for r_ in range(2):
                for k in range(KC):
                    src = gout[j][r_ * GR + k * 128:r_ * GR + (k + 1) * 128, :].rearrange("p (i t) -> p i t", t=TB)
                    dst = ckvT[:, k, :].rearrange("p (i r t) -> p i r t", r=2, t=TB)[:, j * BPP:(j + 1) * BPP, r_, :]
                    prs.append((dst, src))
        self.dma("sp", prs, "os0", [], [ckvd])
        prs = []
        for j in range(NGP):
            for r_ in range(2):
                src = gout[j][r_ * GR + KC * 128:r_ * GR + KC * 128 + 64, :].rearrange("p (i t) -> p i t", t=TB)
                for half in range(2):
                    dst = kpe[half * 64:(half + 1) * 64, :].rearrange("p (i r t) -> p i r t", r=2, t=TB)[:, j * BPP:(j + 1) * BPP, r_, :]
                    prs.append((dst, src))
        self.dma("sp", prs, "os1", [], [kped])
        self.dma("sp", [(wuk_sb, wb["wuk"][0]), (wuv_sb, wb["wuv"][0])], "os2", [wdep[("wuk", 0)], wdep[("wuv", 0)]], [wukd])
        self.dma("pool", [(mask.rearrange("p a t -> p (a t)"), mask_d[:, :])], "mask", [], [maskd])
        hout4 = hout.ap().rearrange("(r p) (c i k) -> r p c i k", r=2, c=LC, i=NTB)
        self.dma("sp", [(hg[:, 0], hout4[0]), (hg[:, 1], hout4[1])], "os3", [], [hgd])
        self.ts("dve", halo[:, :, :, :], hg[:, 0], f1, None, ALU.mult, None, [hgd, halod], [halod])
        if NTB > 1:
            self.stt("dve", halo[:, :, 1:NTB, :], hg[:, 1, :, 0:NTB - 1, :], f0, halo[:, :, 1:NTB, :], ALU.mult, ALU.add,
                     [hgd, halod], [halod])
        wuk4 = wuk_sb.rearrange("p (h k m) -> p h k m", h=NH, k=KC)
        wuv3 = wuv_sb.rearrange("p (k m) -> p k m", k=KC)
        scale = 1.0 / math.sqrt(192.0)
        qi = 0
        lru_args = (us, ggs, yls, halo, halod, cc, cc2, ccd, bda, bdx, bdd, stl, stld, hsel, hseld, pvd)
        def capture(fn):
            real, buf = P.op, []
            P.op = lambda *a, **k: buf.append((a, k))
            try:
                fn()
            finally:
                P.op = real
            return buf
        tiles_per_head = sum(4 * (2 * i + 2) for i in range(NTB))
        pend = []
        rate = [0]

        def drip(n=None):
            n = rate[0] if n is None else n
            for _ in range(min(n, len(pend))):
                a, k = pend.pop(0)
                P.op(*a, **k)
        for h in range(NH):
            if h > 0 and h % 2 == 0:
                drip(len(pend))
                P.fence(True)
            if h < 2:
                final = (h == 1)
                for i in range(NTB):
                    pend += capture(lambda i=i, final=final: self.lru_block(i, *lru_args, final=final))
                rate[0] = -(-len(pend) // tiles_per_head)
            for kb in range(NKB):
                pk, pkd = self.bank("gen")
                for kc in range(KC):
                    self.mm(pk, wuk4[:, h, kc, :], ckvT[:, kc, kb * TB:(kb + 1) * TB], kc == 0, kc == KC - 1,
                            [ckvd, wukd], [pkd])
                self.cp("act" if kb % 2 == 0 else "dve", Kn[:, kb * TB:(kb + 1) * TB], pk, [pkd], [knd])
            for t4 in range(0, NKT, 4):
                pv, pvd_ = self.bank("gen")
                for tq in range(4):
                    t = t4 + tq
                    for kc in range(KC):
                        self.mm(pv[:, tq * 128:(tq + 1) * 128], ckvT[:, kc, t * 128:(t + 1) * 128],
                                wuv3[:, kc, h * 128:(h + 1) * 128], kc == 0, kc == KC - 1, [ckvd, wukd], [pvd_])
                self.cp("act" if (t4 // 4) % 2 == 1 else "dve", Vh[:, t4:t4 + 4, :],
                        pv.rearrange("p (a d) -> p a d", a=4), [pvd_], [vvd])
            ph = (h % 2) * 64
            for i in range(NTB):
                sl = slice(i * TB, (i + 1) * TB)
                qn_, qp_, qd_ = qn_t[qi % 2], qp_t[qi % 2], qd[qi % 2]
                self.dma("sp", [(qn_, qns[:, h, sl]), (qp_, qps[:, h // 2, sl])], "q%d" % (qi % 2), [], [qd_])
                po, pod = self.bank("o")
                pl, pld = self.bank("l")
                NT = 4 * (2 * i + 2)
                sbank = {}

                def qk(t):
                    ps_, psd_ = self.bank("s")
                    ks = slice(t * 128, (t + 1) * 128)
                    self.mm(ps_, Kn[:, ks], qn_, True, False, [knd, qd_], [psd_])
                    self.mm(ps_, kpe[ph:ph + 64, ks], qp_[ph:ph + 64, :], False, True, [kped, qd_], [psd_])
                    sbank[t] = (ps_, psd_)
                qk(0)
                if NT > 1:
                    qk(1)
                for t in range(NT):
                    if t + 2 < NT:
                        qk(t + 2)
                    ps_, psd_ = sbank.pop(t)
                    pt, ptd = Pt[t % 4], Ptd[t % 4]
                    self.act(pt, ps_, AF.Exp, [psd_], [ptd], scale=scale)
                    if t >= NT - 8:
                        self.tt("pool", pt, pt, mask[:, t - (NT - 8), :], ALU.mult, [ptd, maskd], [ptd])
                    self.mm(po[:, :], Vh[:, t, :], pt, t == 0, t == NT - 1, [vvd, ptd], [pod])
                    self.mm(pl[:, :], self.ones[:, :], pt, t == 0, t == NT - 1, [ptd, constd], [pld])
                    drip()
                rc, rcd = rec[qi % 2], recd[qi % 2]
                P.op("dve", lambda e, rc=rc, pl=pl: e.reciprocal(out=rc, in_=pl), [pld], [rcd])
                ot_, otd_ = ot_t[qi % 2], otd[qi % 2]
                self.tt("dve", ot_, po, rc, ALU.mult, [pod, rcd], [otd_])
                self.dma("sp", [(ots[:, h, sl], ot_)], "ot%d" % (qi % 2), [otd_], [])
                qi += 1
            if h < 2:
                drip(len(pend))
            if h == 0:
                self.dma("sp", [(sin_.ap()[:, :], stl[:, :, :, :].rearrange("p c i k -> p (c i k)"))], "os4", [stld], [sind])
                P.op("pool", lambda e: [e.collective_compute("AllGather", ALU.bypass, replica_groups=pairs,
                                                             ins=[sin_.ap().opt()], outs=[sout.ap().opt()])],
                     [sind], [soutd], dma="cc_o3", ndma=1, inc=1)
                sout4 = sout.ap().rearrange("(r p) (c i k) -> r p c i k", r=2, c=LC, i=NTB)
                self.dma("sp", [(sg_[:, 0], sout4[0]), (sg_[:, 1], sout4[1])], "os5", [soutd], [sgd])
                Pg4 = Pg.rearrange("p c (i r) -> p c i r", r=2)
                Hg4 = Hg.rearrange("p c (i r) -> p c i r", r=2)
                for r_ in range(2):
                    self.cp("dve", Hg4[:, :, :, r_], sg_[:, r_, :, :, 0], [sgd], [cd])
                    self.cp("dve", Pg4[:, :, :, r_], sg_[:, r_, :, :, 1], [sgd], [cd])
                P.op("dve", lambda e: e.memset(HS[:, :, :], 0.0), [], [cd])
                for k in range(LC):
                    P.op("dve", lambda e, k=k: e.tensor_tensor_scan(out=HS[:, k, 1:NG + 1], data0=Pg[:, k, :], data1=Hg[:, k, :],
                                                                   initial=0.0, op0=ALU.mult, op1=ALU.add), [cd], [cd])
                HS4 = HS[:, :, 0:NG].rearrange("p c (i r) -> p c i r", r=2)
                self.ts("dve", hsel[:, :, :], HS4[:, :, :, 0], f0, None, ALU.mult, None, [cd, hseld], [hseld])
                self.stt("dve", hsel[:, :, :], HS4[:, :, :, 1], f1, hsel[:, :, :], ALU.mult, ALU.add, [cd, hseld], [hseld])
        self.lru_group = "mm"

        P.fence(True)
        A.off = common_mark
        self.bgroups = {"mm": [0, 1, 2, 3, 4, 5], "sum": [6, 7]}
        self.bpos = {}
        rs_ad = Dep("rs_am")
        mem_f = c3(A.f32(DC * c.MEM), DC)
        mem_n = c3(A.bf(DC * c.MEM), DC)
        wk_sb = A.bf(XH * DC * 128)
        wv_sb = A.bf(DC * XH * 128)
        memd = [Dep("mem%d" % k) for k in range(DC)]
        memnd = [Dep("memn%d" % k) for k in range(DC)]
        wkd, wvd = Dep("wk"), Dep("wv")
        self.dma("sp", [(mem_f, memT[:, :, :])], "os1", [], memd)
        self.dma("sp", [(wk_sb, wb["xwk"][0])], "os2", [wdep[("xwk", 0)]], [wkd])
        self.dma("sp", [(wv_sb, wb["xwv"][0])], "os3", [wdep[("xwv", 0)]], [wvd])
        MEM = c.MEM
        self.rstd_from([mem_f[:, k, :] for k in range(DC)], memd, c.D, MEM, rs_a[:, 0:MEM], rs_ad)
        self.norm_apply(mem_n, memnd, mem_f, memd, col["g_mem"], rs_a[:, 0:MEM], rs_ad, DC)
        wk4 = wk_sb.rearrange("p (h k m) -> p h k m", h=XH, k=DC)
        wv3 = wv_sb.rearrange("p (k m) -> p k m", k=DC)
        for h in range(XH):
            pk, pkd = self.bank("mm")
            for kc in range(DC):
                self.mm(pk[:, 0:MEM], wk4[:, h, kc, :], mem_n[:, kc, :], kc == 0, kc == DC - 1,
                        [wkd, memnd[kc], constd], [pkd])
            self.cp("act", Kx[:, h, :], pk[:, 0:MEM], [pkd], [kxd])
        for t in range(MT):
            pv, pvd_ = self.bank("mm")
            for kc in range(DC):
                self.mm(pv[:, 0:XH * 128], mem_n[:, kc, t * 128:(t + 1) * 128], wv3[:, kc, :], kc == 0, kc == DC - 1,
                        [wvd, memnd[kc]], [pvd_])
            self.cp("dve", Vx[:, t, :], pv[:, 0:XH * 128], [pvd_], [vxd])


        P.fence(True)
        self.wtag = "b"
        A.off = common_mark
        self.bgroups = {"mm": [0, 1, 2, 3, 4, 5], "sum": [6, 7]}
        self.bpos = {}
        xbuf = [c3(A.f32(DC * TB), DC) for _ in range(2)]
        xbd = [[Dep("xc%d_%d" % (j, k)) for k in range(DC)] for j in range(2)]
        ybuf = c3(A.f32(DC * TB), DC)
        ybd = [Dep("yc%d" % k) for k in range(DC)]
        rs_b, rs_bd = A.f32(TB), Dep("rs_b2")
        rs_ad = Dep("rs_a2")
        rcx = [A.f32(TB) for _ in range(2)]
        rcxd = [Dep("rcx%d" % k) for k in range(2)]
        yin = c3(A.bf(DC * TB), DC)
        yind = Dep("yin")
        hb = c3(A.bf(DC * TB), DC)
        hbd = [Dep("hc%d" % k) for k in range(DC)]
        act_t = c3(A.bf(NFC * TB), NFC)
        actd = [Dep("acu%d" % k) for k in range(NFC)]
        qx = c3(A.bf(XH * TB), XH)
        qxd = [Dep("qx%d" % k) for k in range(XH)]
        px = [c3(A.bf(MT * TB), MT) for _ in range(2)]
        pxd = [Dep("px%d" % k) for k in range(2)]
        ox = c3(A.bf(XH * TB), XH)
        oxd = [Dep("ox%d" % k) for k in range(XH)]
        self.guslot = [A.bf(2 * DC * 128) for _ in range(3)]
        self.guslotd = [Dep("gut%d" % k) for k in range(3)]
        self.dnslot = [A.bf(NFC * 128) for _ in range(2)]
        self.dnslotd = [Dep("dnt%d" % k) for k in range(2)]
        wout_sb = A.bf(DC * DC * 128)
        xwq_sb = A.bf(XH * DC * 128)
        xwo_sb = A.bf(DC * XH * 128)
        woutd, xwqd, xwod = Dep("wout"), Dep("xwq"), Dep("xwo")
        self.dma("sp", [(wout_sb, wb["wout"][0])], "os0", [wdep[("wout", 0)]], [woutd])
        self.dma("sp", [(xwq_sb, wb["xwq"][0])], "os1", [wdep[("xwq", 0)]], [xwqd])
        self.dma("sp", [(xwo_sb, wb["xwo"][0])], "os2", [wdep[("xwo", 0)]], [xwod])
        wout4 = wout_sb.rearrange("p (m k n) -> p m k n", m=DC, k=DC)
        xwq4 = xwq_sb.rearrange("p (m k n) -> p m k n", m=XH, k=DC)
        xwo4 = xwo_sb.rearrange("p (m k n) -> p m k n", m=DC, k=XH)
        xscale = 1.0 / math.sqrt(128.0)
        outs = []

        def load_x1(i):
            j = i % 2
            self.dma("sp", [(xbuf[j], x1s[:, :, i * TB:(i + 1) * TB])], "x%d" % j, [], xbd[j])
        load_x1(0)
        for i in range(NTB):
            j = i % 2
            sl = slice(i * TB, (i + 1) * TB)
            xb, xd = xbuf[j], xbd[j]
            if i + 1 < NTB:
                load_x1(i + 1)
            self.dma("sp", [(yin[:, 0:NH, :], ots[:, :, sl]), (yin[:, NH:NH + LC, :], yls[:, :, sl])], "yin", [], [yind])
            for m in range(DC):
                py, pyd = self.bank("mm")
                for kc in range(DC):
                    self.mm(py, wout4[:, m, kc, :], yin[:, kc, :], kc == 0, kc == DC - 1, [woutd, yind], [pyd])
                self.cp("act", ybuf[:, m, :], py, [pyd], [ybd[m]])
            self.rstd_from([ybuf[:, k, :] for k in range(DC)], ybd, c.D, TB, rs_b, rs_bd)
            self.residual(xb, xd, ybuf, ybd, col["g_mixpost"], rs_b, rs_bd, DC, 1.0)
            self.rstd_from([xb[:, k, :] for k in range(DC)], xd, c.D, TB, rs_a, rs_ad)
            self.norm_apply(hb, hbd, xb, xd, col["g_xapre"], rs_a, rs_ad, DC)
            for h in range(XH):
                pq, pqd = self.bank("mm")
                for kc in range(DC):
                    self.mm(pq, xwq4[:, h, kc, :], hb[:, kc, :], kc == 0, kc == DC - 1, [xwqd, hbd[kc]], [pqd])
                self.cp("act", qx[:, h, :], pq, [pqd], [qxd[h]])
            for h in range(XH):
                pp, ppd = px[h % 2], pxd[h % 2]
                for t in range(MT):
                    ps_, psd_ = self.bank("mm")
                    self.mm(ps_, Kx[:, h, t * 128:(t + 1) * 128], qx[:, h, :], True, True, [kxd, qxd[h]], [psd_])
                    self.act(pp[:, t, :], ps_, AF.Exp, [psd_], [ppd], scale=xscale)
                po, pod = self.bank("mm")
                pl, pld = self.bank("sum")
                for t in range(MT):
                    self.mm(po, Vx[:, t, h * 128:(h + 1) * 128], pp[:, t, :], t == 0, t == MT - 1, [vxd, ppd], [pod])
                for t in range(MT):
                    self.mm(pl, self.ones[:, :], pp[:, t, :], t == 0, t == MT - 1, [ppd], [pld])
                rc, rcd = rcx[h % 2], rcxd[h % 2]
                P.op("dve", lambda e, rc=rc, pl=pl: e.reciprocal(out=rc, in_=pl), [pld], [rcd])
                self.tt("dve", ox[:, h, :], po, rc, ALU.mult, [pod, rcd], [oxd[h]])
            for m in range(DC):
                py, pyd = self.bank("mm")
                for kc in range(XH):
                    self.mm(py, xwo4[:, m, kc, :], ox[:, kc, :], kc == 0, kc == XH - 1, [xwod, oxd[kc]], [pyd])
                self.cp("act", ybuf[:, m, :], py, [pyd], [ybd[m]])
            self.rstd_from([ybuf[:, k, :] for k in range(DC)], ybd, c.D, TB, rs_b, rs_bd)
            self.residual(xb, xd, ybuf, ybd, col["g_xapost"], rs_b, rs_bd, DC, 1.0)
            self.rstd_from([xb[:, k, :] for k in range(DC)], xd, c.D, TB, rs_a, rs_ad)
            self.norm_apply(hb, hbd, xb, xd, col["g_f2pre"], rs_a, rs_ad, DC)
            self._ffn_call(hb, hbd, "gu2", "dn2", wb, wdep, act_t, actd, ybuf, ybd, rs_b, rs_bd)
            self.residual(xb, xd, ybuf, ybd, col["g_f2post"], rs_b, rs_bd, DC, 1.0)
            outs.append(self.dma("sp", [(outT[:, :, sl], xb)], "out%d" % j, xd, []))
        P.wait_all("sp", outs)
        block = E(nc.Block())
        P.finalize_and_emit(block)
        es.close()
        return nc

    def _ffn_call(self, hb, hbd, gname, dname, wb, wdep, act_t, actd, ybuf, ybd, rs, rsd):
        class WD:
            def __init__(s, t, name):
                s.t, s.name = t, name

            def __getitem__(s, k):
                return s.t[k]
        c = self.cfg
        gsrc = [wb[gname][k] for k in range(c.NFC)]
        dsrc = [wb[dname][k] for k in range(c.DC)]
        self._gdeps = [wdep[(gname, k)] for k in range(c.NFC)]
        self._ddeps = [wdep[(dname, k)] for k in range(c.DC)]
        self.ffn2(hb, hbd, gsrc, dsrc, act_t, actd, ybuf, ybd, rs, rsd)

    def ffn2(self, h, hd, gsrc, dsrc, act_t, actd, y, yd, rst, rstd_dep):
        c = self.cfg
        DC, NFC = c.DC, c.NFC
        GU = 2 * DC * 128
        DN = NFC * 128
        loads = {}

        def issue(k):
            if k < NFC:
                loads[k] = self.load_w(self.guslot, self.guslotd, self.gucnt + k, GU, gsrc[k], self._gdeps[k], self.wtag + "gu")
            elif k < NFC + DC:
                m = k - NFC
                loads[k] = self.load_w(self.dnslot, self.dnslotd, self.dncnt + m, DN, dsrc[m], self._ddeps[m], self.wtag + "dn")
        issue(0)
        issue(1)
        for k in range(NFC):
            nxt = k + 2
            if nxt < NFC:
                issue(nxt)
            elif nxt == NFC:
                issue(NFC)
            elif nxt == NFC + 1 and DC > 1:
                issue(NFC + 1)
            w, wd = loads.pop(k)
            w4 = w.rearrange("p (g k m) -> p g k m", g=2, k=DC)
            pg, pgd = self.bank("mm")
            pu, pud = self.bank("mm")
            for kc in range(DC):
                self.mm(pg, w4[:, 0, kc, :], h[:, kc, :], kc == 0, kc == DC - 1, [wd, hd[kc]], [pgd])
            for kc in range(DC):
                self.mm(pu, w4[:, 1, kc, :], h[:, kc, :], kc == 0, kc == DC - 1, [wd, hd[kc]], [pud])
            sg, sgd = self.tmpf[k % 2], self.tmpfd[k % 2]
            self.act(sg, pg, AF.Silu, [pgd], [sgd])
            self.tt("dve", act_t[:, k, :], sg, pu, ALU.mult, [sgd, pud], [actd[k]])
        self.gucnt += NFC
        if NFC == 1:
            issue(NFC)
            if DC > 1:
                issue(NFC + 1)
        for m in range(DC):
            w, wd = loads.pop(NFC + m)
            w3 = w.rearrange("p (k m) -> p k m", k=NFC)
            py, pyd = self.bank("mm")
            for kc in range(NFC):
                self.mm(py, w3[:, kc, :], act_t[:, kc, :], kc == 0, kc == NFC - 1, [wd, actd[kc]], [pyd])
            if m + 2 < DC:
                issue(NFC + m + 2)
            self.cp("act", y[:, m, :], py, [pyd], [yd[m]])
        self.dncnt += DC
        self.rstd_from([y[:, m, :] for m in range(DC)], yd, c.D, TB, rst, rstd_dep, half=True)

    def gelu_tanh(self, x_ps, xd, tmp, tmpd, out, outd):
        self.act(tmp, x_ps, AF.Square, [xd], [tmpd])
        self.ts("dve", tmp, tmp, GELU_C, 1.0, ALU.mult, ALU.add, [tmpd], [tmpd])
        self.tt("dve", tmp, tmp, x_ps, ALU.mult, [tmpd, xd], [tmpd])
        self.act(tmp, tmp, AF.Sigmoid, [tmpd], [tmpd], scale=GELU_S)
        self.tt("dve", out, tmp, x_ps, ALU.mult, [tmpd, xd], [outd])

    def lru_alloc(self, A):
        c = self.cfg
        LC = c.LC
        self.l_u = c3(A.f32(LC * (TB + 4)), LC)
        self.l_gg = c3(A.f32(LC * TB), LC)
        self.l_xc = c3(A.f32(LC * TB), LC)
        self.l_r = c3(A.f32(LC * TB), LC)
        self.l_i = c3(A.f32(LC * TB), LC)
        self.l_a = c3(A.f32(LC * TB), LC)
        self.l_b = c3(A.f32(LC * TB), LC)
        self.l_xb = c3(A.bf(LC * TB), LC)
        self.l_y = c3(A.bf(LC * TB), LC)
        self.l_rsum = A.f32(LC * 2)
        self.l_d = {k: [Dep("l_%s%d" % (k, j)) for j in range(LC)] for k in ("u", "gg", "xc", "r", "i", "a", "b", "xb", "y", "rsum")}

    def lru_block(self, i, us, ggs, yls, halo, halod, cc, cc2, ccd, bda, bdx, bdd, stl, stld, hsel, hseld, pvd, final):
        c = self.cfg
        LC, col = c.LC, c.col
        sl = slice(i * TB, (i + 1) * TB)
        d = self.l_d
        u, gg, xc, r_, ig, a_, b_, xb_, y_ = self.l_u, self.l_gg, self.l_xc, self.l_r, self.l_i, self.l_a, self.l_b, self.l_xb, self.l_y
        self.dma("sp", [(u[:, :, 3:3 + TB], us[:, :, sl])], "lu", [], d["u"])
        if final:
            self.dma("sp", [(gg, ggs[:, :, sl])], "lg", [], d["gg"])
        for k in range(LC):
            self.cp("dve", u[:, k, 0:3], halo[:, k, i, :], [halod], [d["u"][k]])
            self.ts("dve", xc[:, k, :], u[:, k, 0:TB], self.pvec[:, col["convw0"] + k:col["convw0"] + k + 1],
                    self.pvec[:, col["convb"] + k:col["convb"] + k + 1], ALU.mult, ALU.add, [d["u"][k], pvd], [d["xc"][k]])
            for tap in range(1, 4):
                self.stt("dve", xc[:, k, :], u[:, k, tap:tap + TB],
                         self.pvec[:, col["convw%d" % tap] + k:col["convw%d" % tap] + k + 1], xc[:, k, :],
                         ALU.mult, ALU.add, [d["u"][k], d["xc"][k]], [d["xc"][k]])
            self.cp("dve", xb_[:, k, :], xc[:, k, :], [d["xc"][k]], [d["xb"][k]])
            pr, prd = self.bank(getattr(self, "lru_group", "mm"))
            pi, pid = self.bank(getattr(self, "lru_group", "mm"))
            self.mm(pr, bda[:, k, :], xb_[:, k, :], True, True, [bdd, d["xb"][k]], [prd])
            self.mm(pi, bdx[:, k, :], xb_[:, k, :], True, True, [bdd, d["xb"][k]], [pid])
            rs = self.l_rsum[:, 2 * k:2 * k + 1]
            P = self.P
            P.op("dve", lambda e, rs=rs: e.memset(rs, 0.0), [], [d["rsum"][k]])
            self.act(r_[:, k, :], pr, AF.Sigmoid, [prd, pvd, d["rsum"][k]], [d["r"][k], d["rsum"][k]],
                     bias=self.pvec[:, col["b_a"] + k:col["b_a"] + k + 1], accum=rs)
            self.act(ig[:, k, :], pi, AF.Sigmoid, [pid, pvd], [d["i"][k]],
                     bias=self.pvec[:, col["b_x"] + k:col["b_x"] + k + 1])
            self.act(a_[:, k, :], r_[:, k, :], AF.Exp, [d["r"][k], ccd], [d["a"][k]], scale=cc[:, k:k + 1])
            self.act(b_[:, k, :], r_[:, k, :], AF.Exp, [d["r"][k], ccd], [d["b"][k]], scale=cc2[:, k:k + 1])
            self.ts("dve", b_[:, k, :], b_[:, k, :], -1.0, 1.0, ALU.mult, ALU.add, [d["b"][k]], [d["b"][k]])
            self.act(b_[:, k, :], b_[:, k, :], AF.Sqrt, [d["b"][k]], [d["b"][k]])
            self.tt("dve", ig[:, k, :], ig[:, k, :], xc[:, k, :], ALU.mult, [d["i"][k], d["xc"][k]], [d["i"][k]])
            self.tt("dve", b_[:, k, :], b_[:, k, :], ig[:, k, :], ALU.mult, [d["b"][k], d["i"][k]], [d["b"][k]])
            if not final:
                P.op("dve", lambda e, k=k: e.tensor_tensor_scan(out=r_[:, k, :], data0=a_[:, k, :], data1=b_[:, k, :],
                                                               initial=0.0, op0=ALU.mult, op1=ALU.add),
                     [d["a"][k], d["b"][k], d["r"][k]], [d["r"][k]])
                self.cp("dve", stl[:, k, i, 0:1], r_[:, k, TB - 1:TB], [d["r"][k], stld], [stld])
                self.act(stl[:, k, i, 1:2], rs, AF.Exp, [d["rsum"][k], ccd, stld], [stld], scale=cc[:, k:k + 1])
            else:
                P.op("dve", lambda e, k=k: e.tensor_tensor_scan(out=r_[:, k, :], data0=a_[:, k, :], data1=b_[:, k, :],
                                                               initial=hsel[:, k, i:i + 1], op0=ALU.mult, op1=ALU.add),
                     [d["a"][k], d["b"][k], d["r"][k], hseld], [d["r"][k]])
                self.tt("dve", y_[:, k, :], r_[:, k, :], gg[:, k, :], ALU.mult, [d["r"][k], d["gg"][k]], [d["y"][k]])
        if final:
            self.dma("sp", [(yls[:, :, sl], y_)], "ly", d["y"], [])

    def arena_elems(self):
        c = self.cfg
        if c.D == 1024:
            return 98000
        return 60000


def fm(a):
    t, f = a.shape
    return np.ascontiguousarray(a.T.reshape(f // 128, 128, t).transpose(1, 0, 2))


def lhs_tiles(w, mc_cols):
    K = w.shape[0]
    kc = K // 128
    out = np.zeros((128, len(mc_cols), kc, 128), np.float32)
    for m, cols in enumerate(mc_cols):
        blk = w[:, cols]
        out[:, m, :, :] = blk.reshape(kc, 128, 128).transpose(1, 0, 2)
    return out


def prep_core(cfg, inp, core):
    c = cfg
    b, r = core // 2, core % 2
    D, DC, NH, LC, QL, KVL, LW, DFF = c.D, c.DC, c.NH, c.LC, c.QL, c.KVL, c.LW, c.DFF
    tok = np.concatenate([np.arange((2 * i + r) * TB, (2 * i + r + 1) * TB) for i in range(c.NTB)])
    m = {}
    m["xT"] = fm(np.asarray(inp["x"][b])[tok])
    m["pos"] = np.ascontiguousarray(np.asarray(inp["positions"][b])[tok].reshape(1, -1)).astype(np.int32)
    m["memT"] = fm(np.asarray(inp["mem"][b]))
    pv = np.zeros((128, c.NV), np.float32)
    col = c.col

    def putvec(name, v):
        v = np.asarray(v, np.float32).reshape(-1, 128).T
        pv[:, col[name]:col[name] + v.shape[1]] = v
    for nm, key in (("g_f1pre", "ffn1_pre_g"), ("g_f1post", "ffn1_post_g"), ("g_mixpre", "mix_pre_g"),
                    ("g_mixpost", "mix_post_g"), ("g_xapre", "xa_pre_g"), ("g_mem", "mem_norm_g"),
                    ("g_xapost", "xa_post_g"), ("g_f2pre", "ffn2_pre_g"), ("g_f2post", "ffn2_post_g"),
                    ("g_qa", "q_a_norm_g"), ("g_kva", "kv_a_norm_g"), ("convb", "conv_b"),
                    ("b_a", "rg_b_a"), ("b_x", "rg_b_x"), ("lam", "rg_lambda")):
        putvec(nm, inp[key][0])
    for tap in range(4):
        putvec("convw%d" % tap, inp["conv_w"][0][tap])
    inv = (np.float32(10000.0) ** (-np.arange(0, 64, 2, dtype=np.float32) / np.float32(64))).astype(np.float32)
    p = np.arange(128)
    pv[:, col["invf"]] = inv[p % 32]
    pv[:, col["nsg"]] = np.where((p % 64) < 32, -1.0, 1.0)
    pv[:, col["f0"]] = 1.0 - r
    pv[:, col["f1"]] = float(r)
    m["pvec"] = pv
    kk = np.arange(128)[:, None, None] + 128 * np.arange(8)[None, :, None]
    qq = np.arange(TB)[None, None, :] + r * TB
    m["mask"] = (kk <= qq).astype(np.float32).reshape(128, 8 * TB)
    return m


def prep_weights(cfg, inp):
    c = cfg
    D, DC, NH, LC, QL, KVL, LW, DFF, XH = c.D, c.DC, c.NH, c.LC, c.QL, c.KVL, c.LW, c.DFF, c.XH
    NFC, QC, KC = c.NFC, c.QC, c.KC
    w = {}
    A = lambda k: np.asarray(inp[k][0], np.float32)
    ar = np.arange
    for tag, gk, dk in (("1", "ffn1_w_gu", "ffn1_w_down"), ("2", "ffn2_w_gu", "ffn2_w_down")):
        wg = A(gk)
        cols = []
        t = lhs_tiles(wg, [ar(k * 128, (k + 1) * 128) for k in range(2 * NFC)])
        g = t[:, 0:NFC]
        u = t[:, NFC:2 * NFC]
        gu = np.stack([g, u], axis=2)
        w["gu" + tag] = np.ascontiguousarray(gu.transpose(1, 0, 2, 3, 4)).reshape(NFC, 128, 2 * DC * 128)
        t = lhs_tiles(A(dk), [ar(k * 128, (k + 1) * 128) for k in range(DC)])
        w["dn" + tag] = np.ascontiguousarray(t.transpose(1, 0, 2, 3)).reshape(DC, 128, NFC * 128)
    win = A("w_in")
    o1, o2, o3, o4 = QL, QL + KVL, QL + KVL + 64, QL + KVL + 64 + LW
    mcs = [ar(k * 128, (k + 1) * 128) for k in range(QC)]
    mcs += [o1 + ar(k * 128, (k + 1) * 128) for k in range(KC)]
    mcs += [np.concatenate([o2 + ar(0, 64), o2 + ar(32, 64), o2 + ar(0, 32)])]
    mcs += [o3 + ar(k * 128, (k + 1) * 128) for k in range(LC)]
    mcs += [o4 + ar(k * 128, (k + 1) * 128) for k in range(LC)]
    w["win"] = lhs_tiles(win, mcs).reshape(1, 128, -1)
    wuq = A("w_uq")
    mcs = [h * 192 + ar(0, 128) for h in range(NH)]
    for hp in range(c.NHP):
        mcs.append(np.concatenate([(2 * hp) * 192 + 128 + ar(0, 64), (2 * hp + 1) * 192 + 128 + ar(0, 64)]))
    sw = np.concatenate([ar(32, 64), ar(0, 32)])
    for hp in range(c.NHP):
        mcs.append(np.concatenate([(2 * hp) * 192 + 128 + sw, (2 * hp + 1) * 192 + 128 + sw]))
    w["wuq"] = lhs_tiles(wuq, mcs).reshape(1, 128, -1)
    wukv = A("w_ukv")
    w["wuk"] = lhs_tiles(wukv, [h * 256 + ar(0, 128) for h in range(NH)]).reshape(1, 128, -1)
    t = lhs_tiles(wukv, [h * 256 + 128 + ar(0, 128) for h in range(NH)])
    w["wuv"] = np.ascontiguousarray(t.transpose(0, 2, 1, 3)).reshape(1, 128, -1)
    w["wout"] = lhs_tiles(A("w_out"), [ar(k * 128, (k + 1) * 128) for k in range(DC)]).reshape(1, 128, -1)
    w["xwq"] = lhs_tiles(A("xa_w_q"), [ar(k * 128, (k + 1) * 128) for k in range(XH)]).reshape(1, 128, -1)
    wkv = A("xa_w_kv")
    w["xwk"] = lhs_tiles(wkv, [ar(k * 128, (k + 1) * 128) for k in range(XH)]).reshape(1, 128, -1)
    t = lhs_tiles(wkv, [XH * 128 + ar(k * 128, (k + 1) * 128) for k in range(XH)])
    w["xwv"] = np.ascontiguousarray(t.transpose(0, 2, 1, 3)).reshape(1, 128, -1)
    w["xwo"] = lhs_tiles(A("xa_w_o"), [ar(k * 128, (k + 1) * 128) for k in range(DC)]).reshape(1, 128, -1)
    for nm, key in (("bda", "rg_w_a"), ("bdx", "rg_w_x")):
        wa = A(key)
        bd = np.zeros((128, LC, 128), np.float32)
        for k in range(LC):
            bd[0:64, k, 0:64] = wa[2 * k]
            bd[64:128, k, 64:128] = wa[2 * k + 1]
        w[nm] = bd.reshape(1, 128, -1)
    return {"w_" + k: np.ascontiguousarray(v, dtype=np.float32) for k, v in w.items()}


def assemble(cfg, outs, B):
    c = cfg
    res = np.zeros((B, c.S, c.D), np.float32)
    for core, o in enumerate(outs):
        b, r = core // 2, core % 2
        o = np.asarray(o)
        tk = o.transpose(2, 1, 0).reshape(c.T, c.D)
        for i in range(c.NTB):
            g = 2 * i + r
            res[b, g * TB:(g + 1) * TB] = tk[i * TB:(i + 1) * TB]
    return res


_CACHE = {}


def kernel(**inputs):
    cfg = Cfg()
    inp = {k: np.asarray(v) for k, v in inputs.items()}
    if "nc" not in _CACHE:
        _CACHE["nc"] = Builder(cfg).build()
    nc = _CACHE["nc"]
    wts = prep_weights(cfg, inp)
    in_maps = []
    for core in range(cfg.n_cores):
        m = prep_core(cfg, inp, core)
        m.update(wts)
        in_maps.append(m)
    res = run_bass_kernel_spmd(nc, in_maps, core_ids=list(range(cfg.n_cores)))
    outs = [r["outT"] for r in res.results]
    return assemble(cfg, outs, inp["x"].shape[0])
```

```python
import math
from contextlib import ExitStack

import numpy as np
import concourse.bass as bass
import concourse.mybir as mybir
from concourse.bass_utils import run_bass_kernel_spmd

F32 = mybir.dt.float32
BF16 = mybir.dt.bfloat16
I32 = mybir.dt.int32
AF = mybir.ActivationFunctionType
ALU = mybir.AluOpType
AX = mybir.AxisListType

ENGS = ("pe", "act", "dve", "pool", "sp")
TB = 512
EPS = 1e-6
GELU_C = 0.044715
GELU_S = 2.0 * math.sqrt(2.0 / math.pi)
TWO_PI = 2.0 * math.pi


class Cfg:
    def __init__(self, D=1024, DFF=2816, NH=4, QL=384, KVL=256, LW=512, MEM=256, XH=4, S=8192,
                 n_cores=8):
        self.D, self.DFF, self.NH, self.QL, self.KVL, self.LW = D, DFF, NH, QL, KVL, LW
        self.MEM, self.XH, self.S, self.n_cores = MEM, XH, S, n_cores
        self.DC = D // 128
        self.NFC = DFF // 128
        self.QC = QL // 128
        self.KC = KVL // 128
        self.LC = LW // 128
        self.NHP = NH // 2
        self.MT = MEM // 128
        self.T = S // 2
        self.NTB = self.T // TB
        self.NKB = S // TB
        self.NKT = S // 128
        self.WINC = self.QC + self.KC + 1 + 2 * self.LC
        self.WUQC = NH + 2 * self.NHP
        assert NH * 128 + LW == D and XH * 128 <= D
        col = {}
        n = 0

        def add(name, w):
            nonlocal n
            col[name] = n
            n += w
        for g in ("g_f1pre", "g_f1post", "g_mixpre", "g_mixpost", "g_xapre", "g_mem", "g_xapost",
                  "g_f2pre", "g_f2post"):
            add(g, self.DC)
        add("g_qa", self.QC)
        add("g_kva", self.KC)
        for tap in range(4):
            add("convw%d" % tap, self.LC)
        add("convb", self.LC)
        add("b_a", self.LC)
        add("b_x", self.LC)
        add("lam", self.LC)
        add("invf", 1)
        add("nsg", 1)
        add("f0", 1)
        add("f1", 1)
        self.col = col
        self.NV = n


class Dep:
    __slots__ = ("name", "ws", "wd", "rs", "rd")

    def __init__(self, name=""):
        self.name = name
        self.ws = {}
        self.wd = []
        self.rs = {}
        self.rd = []


class Op:
    __slots__ = ("eng", "fn", "deps", "sig", "tok", "dma", "ndma", "inc", "ep")


class Prog:
    def __init__(self, nc, es):
        self.nc = nc
        self.es = es
        self.ops = {e: [] for e in ENGS}
        self.all = []
        self.epoch = 0
        self.esems = [{e: es.enter_context(nc.semaphore("s0_" + e)) for e in ENGS}]
        self.dsems = {}
        self.prev_compute = {}

    def dsem(self, name):
        if name not in self.dsems:
            self.dsems[name] = [self.es.enter_context(self.nc.semaphore("d_" + name)), 0]
        return self.dsems[name]

    def op(self, eng, fn, reads=(), writes=(), dma=None, ndma=1, inc=16):
        o = Op()
        o.eng, o.fn, o.sig, o.dma, o.ndma, o.tok = eng, fn, False, dma, ndma, None
        o.inc = inc
        o.ep = self.epoch
        deps = set()
        is_dma = dma is not None
        for d in reads:
            deps.update(d.ws.values())
            deps.update(d.wd)
        for d in writes:
            deps.update(d.ws.values())
            deps.update(d.wd)
            deps.update(d.rs.values())
            deps.update(d.rd)
        if not is_dma and eng == "pe":
            deps = set(d for d in deps if d.dma is not None or d.eng != "pe")
        deps.discard(o)
        o.deps = deps
        for d in reads:
            if is_dma:
                d.rd.append(o)
            else:
                d.rs[eng] = o
        for d in writes:
            if d.rs or d.rd:
                d.ws, d.wd, d.rs, d.rd = {}, [], {}, []
            if is_dma:
                d.wd.append(o)
            else:
                d.ws[eng] = o
        if is_dma:
            s = self.dsem(dma)
            s[1] += inc * ndma
            o.tok = (s[0], s[1])
        self.ops[eng].append(o)
        self.all.append(o)
        return o

    def wait_all(self, eng, ops):
        o = Op()
        o.eng, o.fn, o.sig, o.dma, o.ndma, o.tok = eng, None, False, None, 0, None
        o.inc = 0
        o.ep = self.epoch
        o.deps = set(ops)
        self.ops[eng].append(o)
        self.all.append(o)

    def fence(self, new_epoch=False):
        last = []
        for e in ENGS:
            for o in reversed(self.ops[e]):
                if o.dma is None and o.fn is not None:
                    last.append(o)
                    break
        dmas = {}
        for o in self.all:
            if o.dma is not None and not o.dma.startswith("cast"):
                dmas[o.dma] = o
        for e in ENGS:
            self.wait_all(e, last + list(dmas.values()))
        if not new_epoch:
            return
        self.epoch += 1
        nxt = dict(self.esems[-1])
        for e in ("pe", "act", "dve"):
            nxt[e] = self.es.enter_context(self.nc.semaphore("s%d_%s" % (self.epoch, e)))
        self.esems.append(nxt)

    def finalize_and_emit(self, block):
        for o in self.all:
            for d in o.deps:
                if d.dma is None:
                    d.sig = True
        for e in ENGS:
            n = {}
            for o in self.ops[e]:
                if o.dma is None and o.sig:
                    k = id(self.esems[o.ep][e])
                    n[k] = n.get(k, 0) + 1
                    o.tok = (self.esems[o.ep][e], n[k])
            print("engine", e, "signals per sem", sorted(n.values()))
        decos = {"pe": block.tensor, "act": block.scalar, "dve": block.vector,
                 "pool": block.gpsimd, "sp": block.sync}
        for e in ENGS:
            ops = self.ops[e]

            def body(eng, ops=ops):
                waited = {}
                for o in ops:
                    need = {}
                    for d in o.deps:
                        s, v = d.tok
                        k = id(s)
                        if k not in need or need[k][1] < v:
                            need[k] = (s, v)
                    for k, (s, v) in need.items():
                        if waited.get(k, 0) < v:
                            eng.wait_ge(s, v)
                            waited[k] = v
                    if o.fn is None:
                        continue
                    r = o.fn(eng)
                    if o.dma is not None:
                        assert len(r) == o.ndma
                        for ins in r:
                            ins.then_inc(o.tok[0], o.inc)
                    elif o.sig:
                        r.then_inc(o.tok[0], 1)
            decos[e](body)


class Arena:
    def __init__(self, handle, nelem):
        self.h, self.n, self.off = handle, nelem, 0

    def reset(self):
        self.off = 0

    def bf(self, n):
        n = (n + 1) // 2 * 2
        a = self.off
        self.off += n
        assert self.off <= self.n, ("arena overflow", self.off, self.n)
        return self.h[:, a:a + n]

    def f32(self, n):
        a = self.off
        self.off += 2 * n
        assert self.off <= self.n, ("arena overflow", self.off, self.n)
        return self.h[:, a:a + 2 * n].bitcast(F32)


def c3(ap, c):
    return ap.rearrange("p (c t) -> p c t", c=c)


class Builder:
    def __init__(self, cfg, debug=False):
        self.cfg = cfg
        self.debug = debug

    def mm(self, out, lhsT, rhs, start, stop, reads, writes):
        return self.P.op("pe", lambda e: e.matmul(out, lhsT=lhsT, rhs=rhs, start=start, stop=stop),
                         reads, writes)

    def act(self, out, in_, func, reads, writes, bias=None, scale=1.0, accum=None):
        def fn(e):
            kw = {}
            if bias is not None:
                kw["bias"] = bias
            if accum is not None:
                kw["accum_out"] = accum
            return e.activation(out=out, in_=in_, func=func, scale=scale, **kw)
        return self.P.op("act", fn, reads, writes)

    def tt(self, eng, out, in0, in1, op, reads, writes):
        return self.P.op(eng, lambda e: e.tensor_tensor(out=out, in0=in0, in1=in1, op=op), reads, writes)

    def ts(self, eng, out, in0, s1, s2, op0, op1, reads, writes):
        if s2 is None:
            return self.P.op(eng, lambda e: e.tensor_scalar(out=out, in0=in0, scalar1=s1, scalar2=None, op0=op0),
                             reads, writes)
        return self.P.op(eng, lambda e: e.tensor_scalar(out=out, in0=in0, scalar1=s1, scalar2=s2, op0=op0, op1=op1),
                         reads, writes)

    def stt(self, eng, out, in0, scalar, in1, op0, op1, reads, writes):
        return self.P.op(eng, lambda e: e.scalar_tensor_tensor(out=out, in0=in0, scalar=scalar, in1=in1,
                                                               op0=op0, op1=op1), reads, writes)

    def cp(self, eng, out, in_, reads, writes):
        if eng == "act":
            return self.P.op("act", lambda e: e.activation(out=out, in_=in_, func=AF.Copy), reads, writes)
        return self.P.op(eng, lambda e: e.tensor_copy(out=out, in_=in_), reads, writes)

    def dma(self, q, pairs, sem, reads, writes):
        return self.P.op(q, lambda e: [e.dma_start(out=o, in_=i) for (o, i) in pairs], reads, writes,
                         dma=sem, ndma=len(pairs))

    def bank(self, group):
        lst = self.bgroups[group]
        k = self.bpos.get(group, 0)
        self.bpos[group] = k + 1
        b = lst[k % len(lst)]
        return self.ps[b], self.psd[b]

    def rstd_from(self, srcs, sdeps, nfeat, N, out_rstd, out_dep, half=False):
        ps, pd = self.bank("sum")
        n = len(srcs)
        for c in range(n):
            k = self.sqpos % 3
            self.sqpos += 1
            sq, sqd = self.sq[k][:, 0:N], self.sqd[k]
            self.act(sq, srcs[c], AF.Square, [sdeps[c]], [sqd])
            self.mm(ps[:, 0:N], self.ones[:, :], sq, c == 0, c == n - 1, [sqd, self.constd], [pd])
        sc = (0.25 if half else 1.0)
        self.act(out_rstd, ps[:, 0:N], AF.Sqrt, [pd, self.constd], [out_dep], bias=self.eps4[:, 0:1] if half else self.eps1[:, 0:1],
                 scale=(4.0 if half else 1.0) / nfeat)
        self.P.op("dve", lambda e: e.reciprocal(out=out_rstd, in_=out_rstd), [out_dep], [out_dep])

    def load_w(self, slots, sdeps, k, dst_cols, src_ap, src_dep, semname):
        j = k % len(slots)
        dst = slots[j][:, 0:dst_cols]
        self.dma("sp", [(dst, src_ap)], "%s%d" % (semname, j), [src_dep], [sdeps[j]])
        return dst, sdeps[j]

    def norm_apply(self, out, outd, src, srcd, gcol, rst, rstd_dep, n):
        for k in range(n):
            self.stt("dve", out[:, k, :], src[:, k, :], self.pvec[:, gcol + k:gcol + k + 1], rst,
                     ALU.mult, ALU.mult, [srcd[k], rstd_dep], [outd[k]])

    def residual(self, x, xd, y, yd, gcol, rst, rstd_dep, n, scale):
        for k in range(n):
            self.stt("dve", y[:, k, :], y[:, k, :], self.pvec[:, gcol + k:gcol + k + 1], rst,
                     ALU.mult, ALU.mult, [yd[k], rstd_dep], [yd[k]])
            self.stt("dve", x[:, k, :], y[:, k, :], scale, x[:, k, :], ALU.mult, ALU.add,
                     [yd[k], xd[k]], [xd[k]])

    def build(self):
        c = self.cfg
        nc = bass.Bass("TRN2", target_bir_lowering=False)
        self.nc = nc
        DC, NFC, QC, KC, LC, NH, NHP, MT, XH = c.DC, c.NFC, c.QC, c.KC, c.LC, c.NH, c.NHP, c.MT, c.XH
        T, NTB, S, NKT, NKB = c.T, c.NTB, c.S, c.NKT, c.NKB
        col = c.col

        def din(name, shape, dt=F32):
            return nc.dram_tensor(name, list(shape), dt, kind="ExternalInput")

        def dscr(name, shape, dt, dbg=False):
            if dbg and self.debug:
                return nc.dram_tensor(name, list(shape), dt, kind="ExternalOutput")
            return nc.dram_tensor(name, list(shape), dt)

        xT = din("xT", [128, DC, T])
        pos = din("pos", [1, T], I32)
        memT = din("memT", [128, DC, c.MEM])
        pvec_d = din("pvec", [128, c.NV])
        mask_d = din("mask", [128, 8 * TB])
        wshapes = {
            "gu1": [NFC, 128, 2 * DC * 128], "dn1": [DC, 128, NFC * 128],
            "gu2": [NFC, 128, 2 * DC * 128], "dn2": [DC, 128, NFC * 128],
            "win": [1, 128, c.WINC * DC * 128], "wuq": [1, 128, c.WUQC * QC * 128],
            "wuk": [1, 128, NH * KC * 128], "wuv": [1, 128, KC * NH * 128],
            "wout": [1, 128, DC * DC * 128], "xwq": [1, 128, XH * DC * 128],
            "xwk": [1, 128, XH * DC * 128], "xwv": [1, 128, DC * XH * 128],
            "xwo": [1, 128, DC * XH * 128], "bda": [1, 128, LC * 128], "bdx": [1, 128, LC * 128],
        }
        self.wshapes = wshapes
        wf = {k: din("w_" + k, v) for k, v in wshapes.items()}
        wb = {k: dscr("wb_" + k, v, BF16) for k, v in wshapes.items()}
        outT = nc.dram_tensor("outT", [128, DC, T], F32, kind="ExternalOutput")
        x1s = dscr("x1s", [128, DC, T], F32, dbg=True)
        qns = dscr("qns", [128, NH, T], BF16, dbg=True)
        qps = dscr("qps", [128, NHP, T], BF16, dbg=True)
        us = dscr("us", [128, LC, T], F32, dbg=True)
        ggs = dscr("ggs", [128, LC, T], F32, dbg=True)
        ots = dscr("ots", [128, NH, T], BF16, dbg=True)
        yls = dscr("yls", [128, LC, T], BF16, dbg=True)
        GR = KC * 128 + 64
        NGP = max(1, T // 2048)
        TP = T // NGP
        BPP = TP // TB
        gin = [dscr("gin%d" % j, [GR, TP], BF16) for j in range(NGP)]
        gout = [dscr("gout%d" % j, [2 * GR, TP], BF16) for j in range(NGP)]
        HW_ = LC * NTB * 3
        hin = dscr("hin", [128, HW_], F32)
        hout = dscr("hout", [256, HW_], F32)
        SW_ = LC * NTB * 2
        sin_ = dscr("sin", [128, SW_], F32)
        sout = dscr("sout", [256, SW_], F32, dbg=True)
        pairs = [[2 * k, 2 * k + 1] for k in range(c.n_cores // 2)]

        es = ExitStack()
        self.es = es
        E = es.enter_context
        P = Prog(nc, es)
        self.P = P

        self.pvec = E(nc.sbuf_tensor("pvec_sb", [128, c.NV], F32))
        pvd = Dep("pvec")
        self.ones = E(nc.sbuf_tensor("ones", [128, 128], BF16))
        self.eps1 = E(nc.sbuf_tensor("eps1", [128, 1], F32))
        self.eps4 = E(nc.sbuf_tensor("eps4", [128, 1], F32))
        negpi = E(nc.sbuf_tensor("negpi", [128, 1], F32))
        cc = E(nc.sbuf_tensor("cc", [128, LC], F32))
        cc2 = E(nc.sbuf_tensor("cc2", [128, LC], F32))
        sp_t = E(nc.sbuf_tensor("sp_t", [128, 4 * LC], F32))
        Kx = E(nc.sbuf_tensor("Kx", [128, XH, c.MEM], BF16))
        Vx = E(nc.sbuf_tensor("Vx", [128, MT, XH * 128], BF16))
        bda = E(nc.sbuf_tensor("bda", [128, LC, 128], BF16))
        bdx = E(nc.sbuf_tensor("bdx", [128, LC, 128], BF16))
        halo = E(nc.sbuf_tensor("halo", [128, LC, NTB, 3], F32))
        hsel = E(nc.sbuf_tensor("hsel", [128, LC, NTB], F32))
        stl = E(nc.sbuf_tensor("stl", [128, LC, NTB, 2], F32))
        constd = Dep("const")
        self.constd = constd
        ccd = Dep("cc")
        kxd, vxd, bdd = Dep("kx"), Dep("vx"), Dep("bd")
        halod, hseld, stld = Dep("halo"), Dep("hsel"), Dep("stl")
        ARENA_N = self.arena_elems()
        arena_h = E(nc.sbuf_tensor("arena", [128, ARENA_N], BF16))
        A = Arena(arena_h, ARENA_N)
        self.ps = [E(nc.psum_tensor("ps%d" % i, [128, TB], F32))[:, :] for i in range(8)]
        self.psd = [Dep("ps%d" % i) for i in range(8)]
        self.bpos = {}
        self.sqpos = 0
        self.gucnt = 0
        self.dncnt = 0

        wdep = {}

        P.op("pool", lambda e: e.memset(self.ones[:, :], 1.0), [], [constd])
        P.op("pool", lambda e: e.memset(self.eps1[:, :], EPS), [], [constd])
        P.op("pool", lambda e: e.memset(self.eps4[:, :], 4.0 * EPS), [], [constd])
        P.op("pool", lambda e: e.memset(negpi[:, :], -math.pi), [], [constd])
        P.op("pool", lambda e: e.memset(halo[:, :, :, :], 0.0), [], [halod])
        castslot = [Dep("castslot%d" % k) for k in range(4)]
        self.ncast = 0

        def cast(name, lo=None, hi=None):
            n0 = wshapes[name][0]
            lo = 0 if lo is None else lo
            hi = n0 if hi is None else hi
            d = Dep("w_%s_%d" % (name, lo))
            j = self.ncast % 4
            self.ncast += 1
            self.dma("pool", [(wb[name][lo:hi], wf[name][lo:hi])], "cast%d" % j, [], [d, castslot[j]])
            for k in range(lo, hi):
                wdep[(name, k)] = d
        self.dma("sp", [(self.pvec[:, :], pvec_d[:, :])], "os0", [], [pvd])
        for nm in ("bda", "bdx"):
            cast(nm)
        for lo in range(0, NFC, 2):
            cast("gu1", lo, min(NFC, lo + 2))
        for lo in range(0, DC, 2):
            cast("dn1", lo, min(DC, lo + 2))
        for nm in ("win", "wuq", "wuk", "wuv", "xwk", "xwv", "wout", "xwq", "xwo"):
            cast(nm)
        for lo in range(0, NFC, 2):
            cast("gu2", lo, min(NFC, lo + 2))
        for lo in range(0, DC, 2):
            cast("dn2", lo, min(DC, lo + 2))

        self.bgroups = {"mm": [0, 1, 2, 3, 4, 5], "sum": [6, 7]}

        A.reset()
        self.sq = [A.bf(TB) for _ in range(3)]
        self.sqd = [Dep("sq%d" % i) for i in range(3)]
        self.tmpf = [A.f32(TB) for _ in range(2)]
        self.tmpfd = [Dep("tmpf%d" % i) for i in range(2)]
        rs_a, rs_ad = A.f32(TB), Dep("rs_a")
        common_mark = A.off
        lam = self.pvec[:, col["lam"]:col["lam"] + LC]
        e_ = sp_t[:, 0:LC]
        w_ = sp_t[:, LC:2 * LC]
        l_ = sp_t[:, 2 * LC:3 * LC]
        d_ = sp_t[:, 3 * LC:4 * LC]
        spd = Dep("sp")
        self.act(e_, lam, AF.Exp, [pvd], [spd], scale=-1.0)
        self.ts("dve", w_, e_, 1.0, None, ALU.add, None, [spd], [spd])
        self.act(l_, w_, AF.Ln, [spd], [spd])
        self.ts("dve", d_, w_, -1.0, 1e-30, ALU.add, ALU.max, [spd], [spd])
        P.op("dve", lambda e: e.reciprocal(out=d_, in_=d_), [spd], [spd])
        self.tt("dve", l_, l_, e_, ALU.mult, [spd], [spd])
        self.tt("dve", l_, l_, d_, ALU.mult, [spd], [spd])
        self.ts("dve", cc[:, :], l_, -8.0, None, ALU.mult, None, [spd], [ccd])
        self.ts("dve", cc2[:, :], l_, -16.0, None, ALU.mult, None, [spd], [ccd])

        self.dma("sp", [(bda[:, :, :], wb["bda"][0].rearrange("p (c m) -> p c m", c=LC))], "os4", [wdep[("bda", 0)]], [bdd])
        self.dma("sp", [(bdx[:, :, :], wb["bdx"][0].rearrange("p (c m) -> p c m", c=LC))], "os4", [wdep[("bdx", 0)]], [bdd])

        P.fence()
        self.wtag = "a"
        A.off = common_mark
        xbuf = [c3(A.f32(DC * TB), DC) for _ in range(2)]
        xbd = [[Dep("xb%d_%d" % (j, k)) for k in range(DC)] for j in range(2)]
        ybuf = c3(A.f32(DC * TB), DC)
        ybd = [Dep("yb%d" % k) for k in range(DC)]
        ust = [A.f32(TB) for _ in range(2)]
        ustd = [Dep("ust%d" % k) for k in range(2)]
        ggst = [A.f32(TB) for _ in range(2)]
        ggstd = [Dep("ggst%d" % k) for k in range(2)]
        posi = A.f32(TB).bitcast(I32)
        posf = A.f32(TB)
        tabA, tabB, mtmp = A.f32(TB), A.f32(TB), A.f32(TB)
        ang, kf = A.f32(TB), A.f32(TB)
        ki = A.f32(TB).bitcast(I32)
        tabd, posd = Dep("tab"), Dep("pos")
        rs_b, rs_bd = A.f32(TB), Dep("rs_b")
        rt = [A.f32(TB) for _ in range(2)]
        rtd = [Dep("rt%d" % k) for k in range(2)]
        gt = [A.f32(TB) for _ in range(2)]
        gtd = [Dep("gt%d" % k) for k in range(2)]
        hb = c3(A.bf(DC * TB), DC)
        hbd = [Dep("hb%d" % k) for k in range(DC)]
        act_t = c3(A.bf(NFC * TB), NFC)
        actd = [Dep("act%d" % k) for k in range(NFC)]
        cqn = c3(A.bf(QC * TB), QC)
        cqnd = [Dep("cqn%d" % k) for k in range(QC)]
        ckvn = c3(A.bf(KC * TB), KC)
        ckvnd = [Dep("ckvn%d" % k) for k in range(KC)]
        kr, krd = A.bf(TB), Dep("kr")
        qnst = c3(A.bf(NH * TB), NH)
        qnstd = Dep("qnst")
        qpst = c3(A.bf(NHP * TB), NHP)
        qpstd = Dep("qpst")
        self.guslot = [A.bf(2 * DC * 128) for _ in range(3)]
        self.guslotd = [Dep("gus%d" % k) for k in range(3)]
        self.dnslot = [A.bf(NFC * 128) for _ in range(2)]
        self.dnslotd = [Dep("dns%d" % k) for k in range(2)]
        win_sb = A.bf(c.WINC * DC * 128)
        wuq_sb = A.bf(c.WUQC * QC * 128)
        wind, wuqd = Dep("win"), Dep("wuq")
        win4 = win_sb.rearrange("p (m k n) -> p m k n", m=c.WINC, k=DC)
        wuq4 = wuq_sb.rearrange("p (m k n) -> p m k n", m=c.WUQC, k=QC)
        hin4 = hin.ap().rearrange("p (c i k) -> p c i k", c=LC, i=NTB)

        def load_x(i):
            j = i % 2
            self.dma("sp", [(xbuf[j], xT[:, :, i * TB:(i + 1) * TB])], "x%d" % j, [], xbd[j])

        load_x(0)
        for i in range(NTB):
            j = i % 2
            sl = slice(i * TB, (i + 1) * TB)
            xb, xd = xbuf[j], xbd[j]
            if i + 1 < NTB:
                load_x(i + 1)
            self.dma("sp", [(posi, bass.AP(pos, i * TB, [[0, 128], [1, TB]]))], "pos", [], [posd])
            self.cp("dve", posf, posi, [posd], [posd])
            C1 = 6.28125
            C2 = TWO_PI - 6.28125
            self.ts("dve", ang, posf, self.pvec[:, col["invf"]:col["invf"] + 1], None, ALU.mult, None,
                    [posd, pvd, tabd], [tabd])
            self.ts("dve", ki, ang, 1.0 / TWO_PI, None, ALU.mult, None, [tabd], [tabd])
            self.cp("dve", kf, ki, [tabd], [tabd])
            self.stt("dve", mtmp, kf, -C1, ang, ALU.mult, ALU.add, [tabd], [tabd])
            self.stt("dve", mtmp, kf, -C2, mtmp, ALU.mult, ALU.add, [tabd], [tabd])

            def wrap_sin(dst, src):
                self.ts("dve", kf, src, math.pi, -TWO_PI, ALU.is_gt, ALU.mult, [tabd], [tabd])
                self.tt("dve", src, src, kf, ALU.add, [tabd], [tabd])
                self.ts("dve", src, src, -math.pi, math.pi, ALU.max, ALU.min, [tabd], [tabd])
                self.act(dst, src, AF.Sin, [tabd], [tabd])
            self.ts("dve", ang, mtmp, 0.5 * math.pi, None, ALU.add, None, [tabd], [tabd])
            wrap_sin(tabB, mtmp)
            wrap_sin(tabA, ang)
            self.ts("dve", tabB, tabB, self.pvec[:, col["nsg"]:col["nsg"] + 1], None, ALU.mult, None, [tabd], [tabd])
            self.rstd_from([xb[:, k, :] for k in range(DC)], xd, c.D, TB, rs_a, rs_ad)
            self.norm_apply(hb, hbd, xb, xd, col["g_f1pre"], rs_a, rs_ad, DC)
            self._ffn_call(hb, hbd, "gu1", "dn1", wb, wdep, act_t, actd, ybuf, ybd, rs_b, rs_bd)
            if i == 0:
                self.dma("sp", [(win_sb, wb["win"][0])], "os0", [wdep[("win", 0)]], [wind])
                self.dma("sp", [(wuq_sb, wb["wuq"][0])], "os1", [wdep[("wuq", 0)]], [wuqd])
            self.residual(xb, xd, ybuf, ybd, col["g_f1post"], rs_b, rs_bd, DC, 1.0)
            self.dma("sp", [(x1s[:, :, sl], xb)], "sx%d" % j, xd, [])
            self.rstd_from([xb[:, k, :] for k in range(DC)], xd, c.D, TB, rs_a, rs_ad)
            self.norm_apply(hb, hbd, xb, xd, col["g_mixpre"], rs_a, rs_ad, DC)

            def proj(mc, ncol=128, c0=0):
                pz, pzd = self.bank("mm")
                for kc in range(DC):
                    self.mm(pz[0:ncol, :], win4[:, mc, kc, c0:c0 + ncol], hb[:, kc, :], kc == 0, kc == DC - 1,
                            [wind, hbd[kc]], [pzd])
                return pz, pzd
            for (n_, m0, gname, dst, dstd) in ((QC, 0, "g_qa", cqn, cqnd), (KC, QC, "g_kva", ckvn, ckvnd)):
                for k in range(n_):
                    pz, pzd = proj(m0 + k)
                    self.cp("act", ybuf[:, k, :], pz, [pzd], [ybd[k]])
                self.rstd_from([ybuf[:, k, :] for k in range(n_)], ybd, n_ * 128, TB, rs_b, rs_bd)
                self.norm_apply(dst, dstd, ybuf, ybd, col[gname], rs_b, rs_bd, n_)
            for k in range(KC):
                self.dma("sp", [(gin[i // BPP][k * 128:(k + 1) * 128, (i % BPP) * TB:(i % BPP + 1) * TB], ckvn[:, k, :])],
                         "gi%d" % k, [ckvnd[k]], [])
            mk = QC + KC
            p1, p1d = proj(mk, 64, 0)
            p2, p2d = proj(mk, 64, 64)
            self.tt("dve", rt[0][0:64, :], p1[0:64, :], tabA[0:64, :], ALU.mult, [p1d, tabd], [rtd[0]])
            self.tt("dve", rt[1][0:64, :], p2[0:64, :], tabB[0:64, :], ALU.mult, [p2d, tabd], [rtd[1]])
            self.tt("dve", kr[0:64, :], rt[0][0:64, :], rt[1][0:64, :], ALU.add, [rtd[0], rtd[1]], [krd])
            self.dma("sp", [(gin[i // BPP][KC * 128:KC * 128 + 64, (i % BPP) * TB:(i % BPP + 1) * TB], kr[0:64, :])],
                     "gikr", [krd], [])
            for k in range(LC):
                pz, pzd = proj(mk + 1 + k)
                u_, ud_ = ust[k % 2], ustd[k % 2]
                self.cp("act", u_, pz, [pzd], [ud_])
                self.dma("sp", [(us[:, k, sl], u_), (hin4[:, k, i, :], u_[:, TB - 3:TB])], "us%d" % (k % 2), [ud_], [])
            for k in range(LC):
                pz, pzd = proj(mk + 1 + LC + k)
                self.gelu_tanh(pz, pzd, gt[k % 2], gtd[k % 2], ggst[k % 2], ggstd[k % 2])
                self.dma("sp", [(ggs[:, k, sl], ggst[k % 2])], "gg%d" % (k % 2), [ggstd[k % 2]], [])
            def qproj(mc):
                pz, pzd = self.bank("mm")
                for kc in range(QC):
                    self.mm(pz, wuq4[:, mc, kc, :], cqn[:, kc, :], kc == 0, kc == QC - 1, [wuqd, cqnd[kc]], [pzd])
                return pz, pzd
            for h in range(NH):
                pz, pzd = qproj(h)
                self.cp("act", qnst[:, h, :], pz, [pzd], [qnstd])
            self.dma("sp", [(qns[:, :, sl], qnst)], "qn", [qnstd], [])
            for hp in range(NHP):
                p1, p1d = qproj(NH + hp)
                p2, p2d = qproj(NH + NHP + hp)
                self.tt("dve", rt[0], p1, tabA, ALU.mult, [p1d, tabd], [rtd[0]])
                self.tt("dve", rt[1], p2, tabB, ALU.mult, [p2d, tabd], [rtd[1]])
                self.tt("dve", qpst[:, hp, :], rt[0], rt[1], ALU.add, [rtd[0], rtd[1]], [qpstd])
            self.dma("sp", [(qps[:, :, sl], qpst)], "qp", [qpstd], [])

        P.fence(True)
        o1s = []
        for j in range(NGP):
            o1s.append(P.op("pool", lambda e, j=j: [e.collective_compute("AllGather", ALU.bypass, replica_groups=pairs,
                                                                         ins=[gin[j].ap().opt()], outs=[gout[j].ap().opt()])],
                            [], [], dma="cc_o1_%d" % j, ndma=1, inc=1))
        o2 = P.op("pool", lambda e: [e.collective_compute("AllGather", ALU.bypass, replica_groups=pairs,
                                                          ins=[hin.ap().opt()], outs=[hout.ap().opt()])], [], [],
                  dma="cc_o2", ndma=1, inc=1)
        P.wait_all("pool", o1s + [o2])
        P.fence()

        P.fence(True)
        A.reset()
        self.sq = [A.bf(TB) for _ in range(3)]
        self.tmpf = [A.f32(TB) for _ in range(2)]
        rs_a = A.f32(TB)
        common_mark = A.off
        self.bgroups = {"s": [0, 1, 2], "o": [3, 4], "l": [5], "gen": [7, 0, 1, 2], "lru": [6, 7], "sum": [6, 7],
                        "mm": [0, 1, 2, 3, 4, 5]}
        self.bpos = {}
        self.lru_group = "lru"
        f0 = self.pvec[:, col["f0"]:col["f0"] + 1]
        f1 = self.pvec[:, col["f1"]:col["f1"] + 1]
        ckvT = c3(A.bf(KC * S), KC)
        kpe = A.bf(S)
        Kn = A.bf(S)
        Vh = c3(A.bf(NKT * 128), NKT)
        wuk_sb = A.bf(NH * KC * 128)
        wuv_sb = A.bf(KC * NH * 128)
        qn_t = [A.bf(TB) for _ in range(2)]
        qp_t = [A.bf(TB) for _ in range(2)]
        qd = [Dep("q%d" % k) for k in range(2)]
        Pt = [A.bf(TB) for _ in range(4)]
        Ptd = [Dep("P%d" % k) for k in range(4)]
        mask = c3(A.bf(8 * TB), 8)
        ot_t = [A.bf(TB) for _ in range(2)]
        otd = [Dep("ot%d" % k) for k in range(2)]
        rec = [A.f32(TB) for _ in range(2)]
        recd = [Dep("rec%d" % k) for k in range(2)]
        ckvd, kped, knd, vvd, wukd, maskd = Dep("ckvT"), Dep("kpe"), Dep("Kn"), Dep("Vv"), Dep("wukv"), Dep("mask")
        hg = c3(A.f32(2 * LC * NTB * 3), 2).rearrange("p r (c i k) -> p r c i k", c=LC, i=NTB)
        hgd = Dep("hg")
        self.lru_alloc(A)
        sg_ = c3(A.f32(2 * LC * NTB * 2), 2).rearrange("p r (c i k) -> p r c i k", c=LC, i=NTB)
        sgd, sind, soutd = Dep("sg"), Dep("sin"), Dep("sout")
        NG = 2 * NTB
        Pg = c3(A.f32(LC * NG), LC)
        Hg = c3(A.f32(LC * NG), LC)
        HS = c3(A.f32(LC * (NG + 2)), LC)
        cd = Dep("carry")
        prs = []
        for j in range(NGP):
            for r_ in range(2):
                for k in range(KC):
                    src = gout[j][r_ * GR + k * 128:r_ * GR + (k + 1) * 128, :].rearrange("p (i t) -> p i t", t=TB)
                    dst = ckvT[:, k, :].rearrange("p (i r t) -> p i r t", r=2, t=TB)[:, j * BPP:(j + 1) * BPP, r_, :]
                    prs.append((dst, src))
        self.dma("sp", prs, "os0", [], [ckvd])
        prs = []
        for j in range(NGP):
            for r_ in range(2):
                src = gout[j][r_ * GR + KC * 128:r_ * GR + KC * 128 + 64, :].rearrange("p (i t) -> p i t", t=TB)
                for half in range(2):
                    dst = kpe[half * 64:(half + 1) * 64, :].rearrange("p (i r t) -> p i r t", r=2, t=TB)[:, j * BPP:(j + 1) * BPP, r_, :]
                    prs.append((dst, src))
        self.dma("sp", prs, "os1", [], [kped])
        self.dma("sp", [(wuk_sb, wb["wuk"][0]), (wuv_sb, wb["wuv"][0])], "os2", [wdep[("wuk", 0)], wdep[("wuv", 0)]], [wukd])
        self.dma("pool", [(mask.rearrange("p a t -> p (a t)"), mask_d[:, :])], "mask", [], [maskd])
        hout4 = hout.ap().rearrange("(r p) (c i k) -> r p c i k", r=2, c=LC, i=NTB)
        self.dma("sp", [(hg[:, 0], hout4[0]), (hg[:, 1], hout4[1])], "os3", [], [hgd])
        self.ts("dve", halo[:, :, :, :], hg[:, 0], f1, None, ALU.mult, None, [hgd, halod], [halod])
        if NTB > 1:
            self.stt("dve", halo[:, :, 1:NTB, :], hg[:, 1, :, 0:NTB - 1, :], f0, halo[:, :, 1:NTB, :], ALU.mult, ALU.add,
                     [hgd, halod], [halod])
        wuk4 = wuk_sb.rearrange("p (h k m) -> p h k m", h=NH, k=KC)
        wuv3 = wuv_sb.rearrange("p (k m) -> p k m", k=KC)
        scale = 1.0 / math.sqrt(192.0)
        qi = 0
        lru_args = (us, ggs, yls, halo, halod, cc, cc2, ccd, bda, bdx, bdd, stl, stld, hsel, hseld, pvd)
        def capture(fn):
            real, buf = P.op, []
            P.op = lambda *a, **k: buf.append((a, k))
            try:
                fn()
            finally:
                P.op = real
            return buf
        tiles_per_head = sum(4 * (2 * i + 2) for i in range(NTB))
        pend = []
        rate = [0]

        def drip(n=None):
            n = rate[0] if n is None else n
            for _ in range(min(n, len(pend))):
                a, k = pend.pop(0)
                P.op(*a, **k)
        for h in range(NH):
            if h > 0 and h % 2 == 0:
                drip(len(pend))
                P.fence(True)
            if h < 2:
                final = (h == 1)
                for i in range(NTB):
                    pend += capture(lambda i=i, final=final: self.lru_block(i, *lru_args, final=final))
                rate[0] = -(-len(pend) // tiles_per_head)
            for kb in range(NKB):
                pk, pkd = self.bank("gen")
                for kc in range(KC):
                    self.mm(pk, wuk4[:, h, kc, :], ckvT[:, kc, kb * TB:(kb + 1) * TB], kc == 0, kc == KC - 1,
                            [ckvd, wukd], [pkd])
                self.cp("act" if kb % 2 == 0 else "dve", Kn[:, kb * TB:(kb + 1) * TB], pk, [pkd], [knd])
            for t4 in range(0, NKT, 4):
                pv, pvd_ = self.bank("gen")
                for tq in range(4):
                    t = t4 + tq
                    for kc in range(KC):
                        self.mm(pv[:, tq * 128:(tq + 1) * 128], ckvT[:, kc, t * 128:(t + 1) * 128],
                                wuv3[:, kc, h * 128:(h + 1) * 128], kc == 0, kc == KC - 1, [ckvd, wukd], [pvd_])
                self.cp("act" if (t4 // 4) % 2 == 1 else "dve", Vh[:, t4:t4 + 4, :],
                        pv.rearrange("p (a d) -> p a d", a=4), [pvd_], [vvd])
            ph = (h % 2) * 64
            for i in range(NTB):
                sl = slice(i * TB, (i + 1) * TB)
                qn_, qp_, qd_ = qn_t[qi % 2], qp_t[qi % 2], qd[qi % 2]
                self.dma("sp", [(qn_, qns[:, h, sl]), (qp_, qps[:, h // 2, sl])], "q%d" % (qi % 2), [], [qd_])
                po, pod = self.bank("o")
                pl, pld = self.bank("l")
                NT = 4 * (2 * i + 2)
                sbank = {}

                def qk(t):
                    ps_, psd_ = self.bank("s")
                    ks = slice(t * 128, (t + 1) * 128)
                    self.mm(ps_, Kn[:, ks], qn_, True, False, [knd, qd_], [psd_])
                    self.mm(ps_, kpe[ph:ph + 64, ks], qp_[ph:ph + 64, :], False, True, [kped, qd_], [psd_])
                    sbank[t] = (ps_, psd_)
                qk(0)
                if NT > 1:
                    qk(1)
                for t in range(NT):
                    if t + 2 < NT:
                        qk(t + 2)
                    ps_, psd_ = sbank.pop(t)
                    pt, ptd = Pt[t % 4], Ptd[t % 4]
                    self.act(pt, ps_, AF.Exp, [psd_], [ptd], scale=scale)
                    if t >= NT - 8:
                        self.tt("pool", pt, pt, mask[:, t - (NT - 8), :], ALU.mult, [ptd, maskd], [ptd])
                    self.mm(po[:, :], Vh[:, t, :], pt, t == 0, t == NT - 1, [vvd, ptd], [pod])
                    self.mm(pl[:, :], self.ones[:, :], pt, t == 0, t == NT - 1, [ptd, constd], [pld])
                    drip()
                rc, rcd = rec[qi % 2], recd[qi % 2]
                P.op("dve", lambda e, rc=rc, pl=pl: e.reciprocal(out=rc, in_=pl), [pld], [rcd])
                ot_, otd_ = ot_t[qi % 2], otd[qi % 2]
                self.tt("dve", ot_, po, rc, ALU.mult, [pod, rcd], [otd_])
                self.dma("sp", [(ots[:, h, sl], ot_)], "ot%d" % (qi % 2), [otd_], [])
                qi += 1
            if h < 2:
                drip(len(pend))
            if h == 0:
                self.dma("sp", [(sin_.ap()[:, :], stl[:, :, :, :].rearrange("p c i k -> p (c i k)"))], "os4", [stld], [sind])
                P.op("pool", lambda e: [e.collective_compute("AllGather", ALU.bypass, replica_groups=pairs,
                                                             ins=[sin_.ap().opt()], outs=[sout.ap().opt()])],
                     [sind], [soutd], dma="cc_o3", ndma=1, inc=1)
                sout4 = sout.ap().rearrange("(r p) (c i k) -> r p c i k", r=2, c=LC, i=NTB)
                self.dma("sp", [(sg_[:, 0], sout4[0]), (sg_[:, 1], sout4[1])], "os5", [soutd], [sgd])
                Pg4 = Pg.rearrange("p c (i r) -> p c i r", r=2)
                Hg4 = Hg.rearrange("p c (i r) -> p c i r", r=2)
                for r_ in range(2):
                    self.cp("dve", Hg4[:, :, :, r_], sg_[:, r_, :, :, 0], [sgd], [cd])
                    self.cp("dve", Pg4[:, :, :, r_], sg_[:, r_, :, :, 1], [sgd], [cd])
                P.op("dve", lambda e: e.memset(HS[:, :, :], 0.0), [], [cd])
                for k in range(LC):
                    P.op("dve", lambda e, k=k: e.tensor_tensor_scan(out=HS[:, k, 1:NG + 1], data0=Pg[:, k, :], data1=Hg[:, k, :],
                                                                   initial=0.0, op0=ALU.mult, op1=ALU.add), [cd], [cd])
                HS4 = HS[:, :, 0:NG].rearrange("p c (i r) -> p c i r", r=2)
                self.ts("dve", hsel[:, :, :], HS4[:, :, :, 0], f0, None, ALU.mult, None, [cd, hseld], [hseld])
                self.stt("dve", hsel[:, :, :], HS4[:, :, :, 1], f1, hsel[:, :, :], ALU.mult, ALU.add, [cd, hseld], [hseld])
        self.lru_group = "mm"

        P.fence(True)
        A.off = common_mark
        self.bgroups = {"mm": [0, 1, 2, 3, 4, 5], "sum": [6, 7]}
        self.bpos = {}
        rs_ad = Dep("rs_am")
        mem_f = c3(A.f32(DC * c.MEM), DC)
        mem_n = c3(A.bf(DC * c.MEM), DC)
        wk_sb = A.bf(XH * DC * 128)
        wv_sb = A.bf(DC * XH * 128)
        memd = [Dep("mem%d" % k) for k in range(DC)]
        memnd = [Dep("memn%d" % k) for k in range(DC)]
        wkd, wvd = Dep("wk"), Dep("wv")
        self.dma("sp", [(mem_f, memT[:, :, :])], "os1", [], memd)
        self.dma("sp", [(wk_sb, wb["xwk"][0])], "os2", [wdep[("xwk", 0)]], [wkd])
        self.dma("sp", [(wv_sb, wb["xwv"][0])], "os3", [wdep[("xwv", 0)]], [wvd])
        MEM = c.MEM
        self.rstd_from([mem_f[:, k, :] for k in range(DC)], memd, c.D, MEM, rs_a[:, 0:MEM], rs_ad)
        self.norm_apply(mem_n, memnd, mem_f, memd, col["g_mem"], rs_a[:, 0:MEM], rs_ad, DC)
        wk4 = wk_sb.rearrange("p (h k m) -> p h k m", h=XH, k=DC)
        wv3 = wv_sb.rearrange("p (k m) -> p k m", k=DC)
        for h in range(XH):
            pk, pkd = self.bank("mm")
            for kc in range(DC):
                self.mm(pk[:, 0:MEM], wk4[:, h, kc, :], mem_n[:, kc, :], kc == 0, kc == DC - 1,
                        [wkd, memnd[kc], constd], [pkd])
            self.cp("act", Kx[:, h, :], pk[:, 0:MEM], [pkd], [kxd])
        for t in range(MT):
            pv, pvd_ = self.bank("mm")
            for kc in range(DC):
                self.mm(pv[:, 0:XH * 128], mem_n[:, kc, t * 128:(t + 1) * 128], wv3[:, kc, :], kc == 0, kc == DC - 1,
                        [wvd, memnd[kc]], [pvd_])
            self.cp("dve", Vx[:, t, :], pv[:, 0:XH * 128], [pvd_], [vxd])


        P.fence(True)
        self.wtag = "b"
        A.off = common_mark
        self.bgroups = {"mm": [0, 1, 2, 3, 4, 5], "sum": [6, 7]}
        self.bpos = {}
        xbuf = [c3(A.f32(DC * TB), DC) for _ in range(2)]
        xbd = [[Dep("xc%d_%d" % (j, k)) for k in range(DC)] for j in range(2)]
        ybuf = c3(A.f32(DC * TB), DC)
        ybd = [Dep("yc%d" % k) for k in range(DC)]
        rs_b, rs_bd = A.f32(TB), Dep("rs_b2")
        rs_ad = Dep("rs_a2")
        rcx = [A.f32(TB) for _ in range(2)]
        rcxd = [Dep("rcx%d" % k) for k in range(2)]
        yin = c3(A.bf(DC * TB), DC)
        yind = Dep("yin")
        hb = c3(A.bf(DC * TB), DC)
        hbd = [Dep("hc%d" % k) for k in range(DC)]
        act_t = c3(A.bf(NFC * TB), NFC)
        actd = [Dep("acu%d" % k) for k in range(NFC)]
        qx = c3(A.bf(XH * TB), XH)
        qxd = [Dep("qx%d" % k) for k in range(XH)]
        px = [c3(A.bf(MT * TB), MT) for _ in range(2)]
        pxd = [Dep("px%d" % k) for k in range(2)]
        ox = c3(A.bf(XH * TB), XH)
        oxd = [Dep("ox%d" % k) for k in range(XH)]
        self.guslot = [A.bf(2 * DC * 128) for _ in range(3)]
        self.guslotd = [Dep("gut%d" % k) for k in range(3)]
        self.dnslot = [A.bf(NFC * 128) for _ in range(2)]
        self.dnslotd = [Dep("dnt%d" % k) for k in range(2)]
        wout_sb = A.bf(DC * DC * 128)
        xwq_sb = A.bf(XH * DC * 128)
        xwo_sb = A.bf(DC * XH * 128)
        woutd, xwqd, xwod = Dep("wout"), Dep("xwq"), Dep("xwo")
        self.dma("sp", [(wout_sb, wb["wout"][0])], "os0", [wdep[("wout", 0)]], [woutd])
        self.dma("sp", [(xwq_sb, wb["xwq"][0])], "os1", [wdep[("xwq", 0)]], [xwqd])
        self.dma("sp", [(xwo_sb, wb["xwo"][0])], "os2", [wdep[("xwo", 0)]], [xwod])
        wout4 = wout_sb.rearrange("p (m k n) -> p m k n", m=DC, k=DC)
        xwq4 = xwq_sb.rearrange("p (m k n) -> p m k n", m=XH, k=DC)
        xwo4 = xwo_sb.rearrange("p (m k n) -> p m k n", m=DC, k=XH)
        xscale = 1.0 / math.sqrt(128.0)
        outs = []

        def load_x1(i):
            j = i % 2
            self.dma("sp", [(xbuf[j], x1s[:, :, i * TB:(i + 1) * TB])], "x%d" % j, [], xbd[j])
        load_x1(0)
        for i in range(NTB):
            j = i % 2
            sl = slice(i * TB, (i + 1) * TB)
            xb, xd = xbuf[j], xbd[j]
            if i + 1 < NTB:
                load_x1(i + 1)
            self.dma("sp", [(yin[:, 0:NH, :], ots[:, :, sl]), (yin[:, NH:NH + LC, :], yls[:, :, sl])], "yin", [], [yind])
            for m in range(DC):
                py, pyd = self.bank("mm")
                for kc in range(DC):
                    self.mm(py, wout4[:, m, kc, :], yin[:, kc, :], kc == 0, kc == DC - 1, [woutd, yind], [pyd])
                self.cp("act", ybuf[:, m, :], py, [pyd], [ybd[m]])
            self.rstd_from([ybuf[:, k, :] for k in range(DC)], ybd, c.D, TB, rs_b, rs_bd)
            self.residual(xb, xd, ybuf, ybd, col["g_mixpost"], rs_b, rs_bd, DC, 1.0)
            self.rstd_from([xb[:, k, :] for k in range(DC)], xd, c.D, TB, rs_a, rs_ad)
            self.norm_apply(hb, hbd, xb, xd, col["g_xapre"], rs_a, rs_ad, DC)
            for h in range(XH):
                pq, pqd = self.bank("mm")
                for kc in range(DC):
                    self.mm(pq, xwq4[:, h, kc, :], hb[:, kc, :], kc == 0, kc == DC - 1, [xwqd, hbd[kc]], [pqd])
                self.cp("act", qx[:, h, :], pq, [pqd], [qxd[h]])
            for h in range(XH):
                pp, ppd = px[h % 2], pxd[h % 2]
                for t in range(MT):
                    ps_, psd_ = self.bank("mm")
                    self.mm(ps_, Kx[:, h, t * 128:(t + 1) * 128], qx[:, h, :], True, True, [kxd, qxd[h]], [psd_])
                    self.act(pp[:, t, :], ps_, AF.Exp, [psd_], [ppd], scale=xscale)
                po, pod = self.bank("mm")
                pl, pld = self.bank("sum")
                for t in range(MT):
                    self.mm(po, Vx[:, t, h * 128:(h + 1) * 128], pp[:, t, :], t == 0, t == MT - 1, [vxd, ppd], [pod])
                for t in range(MT):
                    self.mm(pl, self.ones[:, :], pp[:, t, :], t == 0, t == MT - 1, [ppd], [pld])
                rc, rcd = rcx[h % 2], rcxd[h % 2]
                P.op("dve", lambda e, rc=rc, pl=pl: e.reciprocal(out=rc, in_=pl), [pld], [rcd])
                self.tt("dve", ox[:, h, :], po, rc, ALU.mult, [pod, rcd], [oxd[h]])
            for m in range(DC):
                py, pyd = self.bank("mm")
                for kc in range(XH):
                    self.mm(py, xwo4[:, m, kc, :], ox[:, kc, :], kc == 0, kc == XH - 1, [xwod, oxd[kc]], [pyd])
                self.cp("act", ybuf[:, m, :], py, [pyd], [ybd[m]])
            self.rstd_from([ybuf[:, k, :] for k in range(DC)], ybd, c.D, TB, rs_b, rs_bd)
            self.residual(xb, xd, ybuf, ybd, col["g_xapost"], rs_b, rs_bd, DC, 1.0)
            self.rstd_from([xb[:, k, :] for k in range(DC)], xd, c.D, TB, rs_a, rs_ad)
            self.norm_apply(hb, hbd, xb, xd, col["g_f2pre"], rs_a, rs_ad, DC)
            self._ffn_call(hb, hbd, "gu2", "dn2", wb, wdep, act_t, actd, ybuf, ybd, rs_b, rs_bd)
            self.residual(xb, xd, ybuf, ybd, col["g_f2post"], rs_b, rs_bd, DC, 1.0)
            outs.append(self.dma("sp", [(outT[:, :, sl], xb)], "out%d" % j, xd, []))
        P.wait_all("sp", outs)
        block = E(nc.Block())
        P.finalize_and_emit(block)
        es.close()
        return nc

    def _ffn_call(self, hb, hbd, gname, dname, wb, wdep, act_t, actd, ybuf, ybd, rs, rsd):
        class WD:
            def __init__(s, t, name):
                s.t, s.name = t, name

            def __getitem__(s, k):
                return s.t[k]
        c = self.cfg
        gsrc = [wb[gname][k] for k in range(c.NFC)]
        dsrc = [wb[dname][k] for k in range(c.DC)]
        self._gdeps = [wdep[(gname, k)] for k in range(c.NFC)]
        self._ddeps = [wdep[(dname, k)] for k in range(c.DC)]
        self.ffn2(hb, hbd, gsrc, dsrc, act_t, actd, ybuf, ybd, rs, rsd)

    def ffn2(self, h, hd, gsrc, dsrc, act_t, actd, y, yd, rst, rstd_dep):
        c = self.cfg
        DC, NFC = c.DC, c.NFC
        GU = 2 * DC * 128
        DN = NFC * 128
        loads = {}

        def issue(k):
            if k < NFC:
                loads[k] = self.load_w(self.guslot, self.guslotd, self.gucnt + k, GU, gsrc[k], self._gdeps[k], self.wtag + "gu")
            elif k < NFC + DC:
                m = k - NFC
                loads[k] = self.load_w(self.dnslot, self.dnslotd, self.dncnt + m, DN, dsrc[m], self._ddeps[m], self.wtag + "dn")
        issue(0)
        issue(1)
        for k in range(NFC):
            nxt = k + 2
            if nxt < NFC:
                issue(nxt)
            elif nxt == NFC:
                issue(NFC)
            elif nxt == NFC + 1 and DC > 1:
                issue(NFC + 1)
            w, wd = loads.pop(k)
            w4 = w.rearrange("p (g k m) -> p g k m", g=2, k=DC)
            pg, pgd = self.bank("mm")
            pu, pud = self.bank("mm")
            for kc in range(DC):
                self.mm(pg, w4[:, 0, kc, :], h[:, kc, :], kc == 0, kc == DC - 1, [wd, hd[kc]], [pgd])
            for kc in range(DC):
                self.mm(pu, w4[:, 1, kc, :], h[:, kc, :], kc == 0, kc == DC - 1, [wd, hd[kc]], [pud])
            sg, sgd = self.tmpf[k % 2], self.tmpfd[k % 2]
            self.act(sg, pg, AF.Silu, [pgd], [sgd])
            self.tt("dve", act_t[:, k, :], sg, pu, ALU.mult, [sgd, pud], [actd[k]])
        self.gucnt += NFC
        if NFC == 1:
            issue(NFC)
            if DC > 1:
                issue(NFC + 1)
        for m in range(DC):
            w, wd = loads.pop(NFC + m)
            w3 = w.rearrange("p (k m) -> p k m", k=NFC)
            py, pyd = self.bank("mm")
            for kc in range(NFC):
                self.mm(py, w3[:, kc, :], act_t[:, kc, :], kc == 0, kc == NFC - 1, [wd, actd[kc]], [pyd])
            if m + 2 < DC:
                issue(NFC + m + 2)
            self.cp("act", y[:, m, :], py, [pyd], [yd[m]])
        self.dncnt += DC
        self.rstd_from([y[:, m, :] for m in range(DC)], yd, c.D, TB, rst, rstd_dep, half=True)

    def gelu_tanh(self, x_ps, xd, tmp, tmpd, out, outd):
        self.act(tmp, x_ps, AF.Square, [xd], [tmpd])
        self.ts("dve", tmp, tmp, GELU_C, 1.0, ALU.mult, ALU.add, [tmpd], [tmpd])
        self.tt("dve", tmp, tmp, x_ps, ALU.mult, [tmpd, xd], [tmpd])
        self.act(tmp, tmp, AF.Sigmoid, [tmpd], [tmpd], scale=GELU_S)
        self.tt("dve", out, tmp, x_ps, ALU.mult, [tmpd, xd], [outd])

    def lru_alloc(self, A):
        c = self.cfg
        LC = c.LC
        self.l_u = c3(A.f32(LC * (TB + 4)), LC)
        self.l_gg = c3(A.f32(LC * TB), LC)
        self.l_xc = c3(A.f32(LC * TB), LC)
        self.l_r = c3(A.f32(LC * TB), LC)
        self.l_i = c3(A.f32(LC * TB), LC)
        self.l_a = c3(A.f32(LC * TB), LC)
        self.l_b = c3(A.f32(LC * TB), LC)
        self.l_xb = c3(A.bf(LC * TB), LC)
        self.l_y = c3(A.bf(LC * TB), LC)
        self.l_rsum = A.f32(LC * 2)
        self.l_d = {k: [Dep("l_%s%d" % (k, j)) for j in range(LC)] for k in ("u", "gg", "xc", "r", "i", "a", "b", "xb", "y", "rsum")}

    def lru_block(self, i, us, ggs, yls, halo, halod, cc, cc2, ccd, bda, bdx, bdd, stl, stld, hsel, hseld, pvd, final):
        c = self.cfg
        LC, col = c.LC, c.col
        sl = slice(i * TB, (i + 1) * TB)
        d = self.l_d
        u, gg, xc, r_, ig, a_, b_, xb_, y_ = self.l_u, self.l_gg, self.l_xc, self.l_r, self.l_i, self.l_a, self.l_b, self.l_xb, self.l_y
        self.dma("sp", [(u[:, :, 3:3 + TB], us[:, :, sl])], "lu", [], d["u"])
        if final:
            self.dma("sp", [(gg, ggs[:, :, sl])], "lg", [], d["gg"])
        for k in range(LC):
            self.cp("dve", u[:, k, 0:3], halo[:, k, i, :], [halod], [d["u"][k]])
            self.ts("dve", xc[:, k, :], u[:, k, 0:TB], self.pvec[:, col["convw0"] + k:col["convw0"] + k + 1],
                    self.pvec[:, col["convb"] + k:col["convb"] + k + 1], ALU.mult, ALU.add, [d["u"][k], pvd], [d["xc"][k]])
            for tap in range(1, 4):
                self.stt("dve", xc[:, k, :], u[:, k, tap:tap + TB],
                         self.pvec[:, col["convw%d" % tap] + k:col["convw%d" % tap] + k + 1], xc[:, k, :],
                         ALU.mult, ALU.add, [d["u"][k], d["xc"][k]], [d["xc"][k]])
            self.cp("dve", xb_[:, k, :], xc[:, k, :], [d["xc"][k]], [d["xb"][k]])
            pr, prd = self.bank(getattr(self, "lru_group", "mm"))
            pi, pid = self.bank(getattr(self, "lru_group", "mm"))
            self.mm(pr, bda[:, k, :], xb_[:, k, :], True, True, [bdd, d["xb"][k]], [prd])
            self.mm(pi, bdx[:, k, :], xb_[:, k, :], True, True, [bdd, d["xb"][k]], [pid])
            rs = self.l_rsum[:, 2 * k:2 * k + 1]
            P = self.P
            P.op("dve", lambda e, rs=rs: e.memset(rs, 0.0), [], [d["rsum"][k]])
            self.act(r_[:, k, :], pr, AF.Sigmoid, [prd, pvd, d["rsum"][k]], [d["r"][k], d["rsum"][k]],
                     bias=self.pvec[:, col["b_a"] + k:col["b_a"] + k + 1], accum=rs)
            self.act(ig[:, k, :], pi, AF.Sigmoid, [pid, pvd], [d["i"][k]],
                     bias=self.pvec[:, col["b_x"] + k:col["b_x"] + k + 1])
            self.act(a_[:, k, :], r_[:, k, :], AF.Exp, [d["r"][k], ccd], [d["a"][k]], scale=cc[:, k:k + 1])
            self.act(b_[:, k, :], r_[:, k, :], AF.Exp, [d["r"][k], ccd], [d["b"][k]], scale=cc2[:, k:k + 1])
            self.ts("dve", b_[:, k, :], b_[:, k, :], -1.0, 1.0, ALU.mult, ALU.add, [d["b"][k]], [d["b"][k]])
            self.act(b_[:, k, :], b_[:, k, :], AF.Sqrt, [d["b"][k]], [d["b"][k]])
            self.tt("dve", ig[:, k, :], ig[:, k, :], xc[:, k, :], ALU.mult, [d["i"][k], d["xc"][k]], [d["i"][k]])
            self.tt("dve", b_[:, k, :], b_[:, k, :], ig[:, k, :], ALU.mult, [d["b"][k], d["i"][k]], [d["b"][k]])
            if not final:
                P.op("dve", lambda e, k=k: e.tensor_tensor_scan(out=r_[:, k, :], data0=a_[:, k, :], data1=b_[:, k, :],
                                                               initial=0.0, op0=ALU.mult, op1=ALU.add),
                     [d["a"][k], d["b"][k], d["r"][k]], [d["r"][k]])
                self.cp("dve", stl[:, k, i, 0:1], r_[:, k, TB - 1:TB], [d["r"][k], stld], [stld])
                self.act(stl[:, k, i, 1:2], rs, AF.Exp, [d["rsum"][k], ccd, stld], [stld], scale=cc[:, k:k + 1])
            else:
                P.op("dve", lambda e, k=k: e.tensor_tensor_scan(out=r_[:, k, :], data0=a_[:, k, :], data1=b_[:, k, :],
                                                               initial=hsel[:, k, i:i + 1], op0=ALU.mult, op1=ALU.add),
                     [d["a"][k], d["b"][k], d["r"][k], hseld], [d["r"][k]])
                self.tt("dve", y_[:, k, :], r_[:, k, :], gg[:, k, :], ALU.mult, [d["r"][k], d["gg"][k]], [d["y"][k]])
        if final:
            self.dma("sp", [(yls[:, :, sl], y_)], "ly", d["y"], [])

    def arena_elems(self):
        c = self.cfg
        if c.D == 1024:
            return 98000
        return 60000


def fm(a):
    t, f = a.shape
    return np.ascontiguousarray(a.T.reshape(f // 128, 128, t).transpose(1, 0, 2))


def lhs_tiles(w, mc_cols):
    K = w.shape[0]
    kc = K // 128
    out = np.zeros((128, len(mc_cols), kc, 128), np.float32)
    for m, cols in enumerate(mc_cols):
        blk = w[:, cols]
        out[:, m, :, :] = blk.reshape(kc, 128, 128).transpose(1, 0, 2)
    return out


def prep_core(cfg, inp, core):
    c = cfg
    b, r = core // 2, core % 2
    D, DC, NH, LC, QL, KVL, LW, DFF = c.D, c.DC, c.NH, c.LC, c.QL, c.KVL, c.LW, c.DFF
    tok = np.concatenate([np.arange((2 * i + r) * TB, (2 * i + r + 1) * TB) for i in range(c.NTB)])
    m = {}
    m["xT"] = fm(np.asarray(inp["x"][b])[tok])
    m["pos"] = np.ascontiguousarray(np.asarray(inp["positions"][b])[tok].reshape(1, -1)).astype(np.int32)
    m["memT"] = fm(np.asarray(inp["mem"][b]))
    pv = np.zeros((128, c.NV), np.float32)
    col = c.col

    def putvec(name, v):
        v = np.asarray(v, np.float32).reshape(-1, 128).T
        pv[:, col[name]:col[name] + v.shape[1]] = v
    for nm, key in (("g_f1pre", "ffn1_pre_g"), ("g_f1post", "ffn1_post_g"), ("g_mixpre", "mix_pre_g"),
                    ("g_mixpost", "mix_post_g"), ("g_xapre", "xa_pre_g"), ("g_mem", "mem_norm_g"),
                    ("g_xapost", "xa_post_g"), ("g_f2pre", "ffn2_pre_g"), ("g_f2post", "ffn2_post_g"),
                    ("g_qa", "q_a_norm_g"), ("g_kva", "kv_a_norm_g"), ("convb", "conv_b"),
                    ("b_a", "rg_b_a"), ("b_x", "rg_b_x"), ("lam", "rg_lambda")):
        putvec(nm, inp[key][0])
    for tap in range(4):
        putvec("convw%d" % tap, inp["conv_w"][0][tap])
    inv = (np.float32(10000.0) ** (-np.arange(0, 64, 2, dtype=np.float32) / np.float32(64))).astype(np.float32)
    p = np.arange(128)
    pv[:, col["invf"]] = inv[p % 32]
    pv[:, col["nsg"]] = np.where((p % 64) < 32, -1.0, 1.0)
    pv[:, col["f0"]] = 1.0 - r
    pv[:, col["f1"]] = float(r)
    m["pvec"] = pv
    kk = np.arange(128)[:, None, None] + 128 * np.arange(8)[None, :, None]
    qq = np.arange(TB)[None, None, :] + r * TB
    m["mask"] = (kk <= qq).astype(np.float32).reshape(128, 8 * TB)
    return m


def prep_weights(cfg, inp):
    c = cfg
    D, DC, NH, LC, QL, KVL, LW, DFF, XH = c.D, c.DC, c.NH, c.LC, c.QL, c.KVL, c.LW, c.DFF, c.XH
    NFC, QC, KC = c.NFC, c.QC, c.KC
    w = {}
    A = lambda k: np.asarray(inp[k][0], np.float32)
    ar = np.arange
    for tag, gk, dk in (("1", "ffn1_w_gu", "ffn1_w_down"), ("2", "ffn2_w_gu", "ffn2_w_down")):
        wg = A(gk)
        cols = []
        t = lhs_tiles(wg, [ar(k * 128, (k + 1) * 128) for k in range(2 * NFC)])
        g = t[:, 0:NFC]
        u = t[:, NFC:2 * NFC]
        gu = np.stack([g, u], axis=2)
        w["gu" + tag] = np.ascontiguousarray(gu.transpose(1, 0, 2, 3, 4)).reshape(NFC, 128, 2 * DC * 128)
        t = lhs_tiles(A(dk), [ar(k * 128, (k + 1) * 128) for k in range(DC)])
        w["dn" + tag] = np.ascontiguousarray(t.transpose(1, 0, 2, 3)).reshape(DC, 128, NFC * 128)
    win = A("w_in")
    o1, o2, o3, o4 = QL, QL + KVL, QL + KVL + 64, QL + KVL + 64 + LW
    mcs = [ar(k * 128, (k + 1) * 128) for k in range(QC)]
    mcs += [o1 + ar(k * 128, (k + 1) * 128) for k in range(KC)]
    mcs += [np.concatenate([o2 + ar(0, 64), o2 + ar(32, 64), o2 + ar(0, 32)])]
    mcs += [o3 + ar(k * 128, (k + 1) * 128) for k in range(LC)]
    mcs += [o4 + ar(k * 128, (k + 1) * 128) for k in range(LC)]
    w["win"] = lhs_tiles(win, mcs).reshape(1, 128, -1)
    wuq = A("w_uq")
    mcs = [h * 192 + ar(0, 128) for h in range(NH)]
    for hp in range(c.NHP):
        mcs.append(np.concatenate([(2 * hp) * 192 + 128 + ar(0, 64), (2 * hp + 1) * 192 + 128 + ar(0, 64)]))
    sw = np.concatenate([ar(32, 64), ar(0, 32)])
    for hp in range(c.NHP):
        mcs.append(np.concatenate([(2 * hp) * 192 + 128 + sw, (2 * hp + 1) * 192 + 128 + sw]))
    w["wuq"] = lhs_tiles(wuq, mcs).reshape(1, 128, -1)
    wukv = A("w_ukv")
    w["wuk"] = lhs_tiles(wukv, [h * 256 + ar(0, 128) for h in range(NH)]).reshape(1, 128, -1)
    t = lhs_tiles(wukv, [h * 256 + 128 + ar(0, 128) for h in range(NH)])
    w["wuv"] = np.ascontiguousarray(t.transpose(0, 2, 1, 3)).reshape(1, 128, -1)
    w["wout"] = lhs_tiles(A("w_out"), [ar(k * 128, (k + 1) * 128) for k in range(DC)]).reshape(1, 128, -1)
    w["xwq"] = lhs_tiles(A("xa_w_q"), [ar(k * 128, (k + 1) * 128) for k in range(XH)]).reshape(1, 128, -1)
    wkv = A("xa_w_kv")
    w["xwk"] = lhs_tiles(wkv, [ar(k * 128, (k + 1) * 128) for k in range(XH)]).reshape(1, 128, -1)
    t = lhs_tiles(wkv, [XH * 128 + ar(k * 128, (k + 1) * 128) for k in range(XH)])
    w["xwv"] = np.ascontiguousarray(t.transpose(0, 2, 1, 3)).reshape(1, 128, -1)
    w["xwo"] = lhs_tiles(A("xa_w_o"), [ar(k * 128, (k + 1) * 128) for k in range(DC)]).reshape(1, 128, -1)
    for nm, key in (("bda", "rg_w_a"), ("bdx", "rg_w_x")):
        wa = A(key)
        bd = np.zeros((128, LC, 128), np.float32)
        for k in range(LC):
            bd[0:64, k, 0:64] = wa[2 * k]
            bd[64:128, k, 64:128] = wa[2 * k + 1]
        w[nm] = bd.reshape(1, 128, -1)
    return {"w_" + k: np.ascontiguousarray(v, dtype=np.float32) for k, v in w.items()}


def assemble(cfg, outs, B):
    c = cfg
    res = np.zeros((B, c.S, c.D), np.float32)
    for core, o in enumerate(outs):
        b, r = core // 2, core % 2
        o = np.asarray(o)
        tk = o.transpose(2, 1, 0).reshape(c.T, c.D)
        for i in range(c.NTB):
            g = 2 * i + r
            res[b, g * TB:(g + 1) * TB] = tk[i * TB:(i + 1) * TB]
    return res


_CACHE = {}


def kernel(**inputs):
    cfg = Cfg()
    inp = {k: np.asarray(v) for k, v in inputs.items()}
    if "nc" not in _CACHE:
        _CACHE["nc"] = Builder(cfg).build()
    nc = _CACHE["nc"]
    wts = prep_weights(cfg, inp)
    in_maps = []
    for core in range(cfg.n_cores):
        m = prep_core(cfg, inp, core)
        m.update(wts)
        in_maps.append(m)
    res = run_bass_kernel_spmd(nc, in_maps, core_ids=list(range(cfg.n_cores)))
    outs = [r["outT"] for r in res.results]
    return assemble(cfg, outs, inp["x"].shape[0])
```
